# Optimizing a Trainium2 kernel written in Bass

```python
import jax, jax.numpy as jnp
from jax import lax
import numpy as np

D_MODEL = 2048
BATCH = 2
SEQ = 8192
DEPTH = 1

HEAD_DIM = 128
N_HEADS_DIL = 8
N_HEADS_SB = 8
DIL_PATTERNS = ((128, 1), (512, 4), (2048, 16))
BLOCK = 128
ROT_DIM = HEAD_DIM // 4
ROPE_THETA = 500000.0
D_FF = 4 * D_MODEL
PLE_DIM = 256
EPS = 1e-6
W_DIL = N_HEADS_DIL * HEAD_DIM
W_SB = N_HEADS_SB * HEAD_DIM
D_IN = 3 * W_DIL + 3 * W_SB + 2 * D_MODEL

kernel_name = "hybrid_dilated_stickbreaking_gated_block"


def rmsnorm(x, g):
    xf = x.astype(jnp.float32)
    y = xf * lax.rsqrt(jnp.mean(xf * xf, axis=-1, keepdims=True) + EPS)
    return (y * g.astype(jnp.float32)).astype(x.dtype)


def partial_rope(x, pos):
    half = ROT_DIM // 2
    inv = ROPE_THETA ** (-jnp.arange(0, ROT_DIM, 2, dtype=jnp.float32) / ROT_DIM)
    ang = pos[:, None] * inv[None, :]
    cos = jnp.cos(ang)[None, :, None, :]
    sin = jnp.sin(ang)[None, :, None, :]
    xr = x[..., :ROT_DIM].astype(jnp.float32)
    x1, x2 = xr[..., :half], xr[..., half:]
    rot = jnp.concatenate([x1 * cos - x2 * sin, x2 * cos + x1 * sin], axis=-1)
    return jnp.concatenate([rot.astype(x.dtype), x[..., ROT_DIM:]], axis=-1)


def dilated_window(q, k, v, window, dilation):
    B, S, H, Dh = q.shape
    W = window // dilation
    assert W <= BLOCK
    M = S // dilation
    Mp = -(-M // BLOCK) * BLOCK
    nb = Mp // BLOCK

    def to_blocks(t):
        t = t.astype(jnp.float32).reshape(B, M, dilation, H, Dh).transpose(0, 2, 3, 1, 4)
        t = jnp.pad(t, ((0, 0), (0, 0), (0, 0), (0, Mp - M), (0, 0)))
        return t.reshape(B, dilation, H, nb, BLOCK, Dh)

    qb, kb, vb = to_blocks(q), to_blocks(k), to_blocks(v)

    def with_prev(t):
        prev = jnp.pad(t, ((0, 0), (0, 0), (0, 0), (1, 0), (0, 0), (0, 0)))[:, :, :, :-1]
        return jnp.concatenate([prev, t], axis=-2)

    kw, vw = with_prev(kb), with_prev(vb)
    s = jnp.einsum('bdhnqe,bdhnke->bdhnqk', qb, kw) * (HEAD_DIM ** -0.5)
    n_i = jnp.arange(nb)[:, None, None]
    q_i = jnp.arange(BLOCK)[None, :, None]
    k_i = jnp.arange(2 * BLOCK)[None, None, :]
    dist = BLOCK + q_i - k_i
    valid = (dist >= 0) & (dist <= W) & ((n_i > 0) | (k_i >= BLOCK))
    s = jnp.where(valid, s, -jnp.inf)
    m = jnp.max(s, axis=-1, keepdims=True)
    e = jnp.exp(s - m)
    den = jnp.sum(e, axis=-1, keepdims=True)
    o = jnp.einsum('bdhnqk,bdhnke->bdhnqe', e, vw) / den
    lse = (m + jnp.log(den))[..., 0]

    o = o.reshape(B, dilation, H, Mp, Dh)[:, :, :, :M].transpose(0, 3, 1, 2, 4).reshape(B, S, H, Dh)
    lse = lse.reshape(B, dilation, H, Mp)[:, :, :, :M].transpose(0, 3, 1, 2).reshape(B, S, H)
    return o, lse


def dilated_mixture(q, k, v):
    outs, lses = [], []
    for window, dilation in DIL_PATTERNS:
        o, l = dilated_window(q, k, v, window, dilation)
        outs.append(o)
        lses.append(l)
    o = jnp.stack(outs, axis=0)
    w = jax.nn.softmax(jnp.stack(lses, axis=0), axis=0)
    return jnp.sum(w[..., None] * o, axis=0)


def stick_breaking(q, k, v):
    B, S, H, Dh = q.shape
    nq = S // BLOCK
    qf = q.astype(jnp.float32).transpose(0, 2, 1, 3)
    kf = k.astype(jnp.float32).transpose(0, 2, 1, 3)
    vf = v.astype(jnp.float32).transpose(0, 2, 1, 3)
    qblocks = qf.reshape(B, H, nq, BLOCK, Dh).transpose(2, 0, 1, 3, 4)
    key_pos = jnp.arange(S)

    def one_block(args):
        qb, n = args
        z = jnp.einsum('bhqe,bhke->bhqk', qb, kf) * (HEAD_DIM ** -0.5)
        q_pos = n * BLOCK + jnp.arange(BLOCK)
        causal = key_pos[None, :] < q_pos[:, None]
        log_1mb = jnp.where(causal, jax.nn.log_sigmoid(-z), 0.0)
        after = lax.cumsum(log_1mb, axis=3, reverse=True) - log_1mb
        a = jnp.where(causal, jnp.exp(jax.nn.log_sigmoid(z) + after), 0.0)
        return jnp.einsum('bhqk,bhke->bhqe', a, vf)

    o = lax.map(one_block, (qblocks, jnp.arange(nq)))
    return o.transpose(1, 0, 3, 2, 4).reshape(B, S, H, Dh)


def setup_inputs(seed: int = 0) -> dict:
    key = jax.random.key(seed)
    ks = jax.random.split(key, 16)

    def w(k, shape, fan_in):
        return jax.random.normal(k, shape, jnp.float32) * (fan_in ** -0.5)

    def gain(k, shape):
        return 1.0 + 0.05 * jax.random.normal(k, shape, jnp.float32)

    return {
        "x": jax.random.normal(ks[0], (BATCH, SEQ, D_MODEL), jnp.float32),
        "p": jax.random.normal(ks[1], (DEPTH, BATCH, SEQ, PLE_DIM), jnp.float32),
        "g_mix": gain(ks[2], (DEPTH, D_MODEL)),
        "w_in": w(ks[3], (DEPTH, D_MODEL, D_IN), D_MODEL),
        "qn_gain": gain(ks[4], (DEPTH, HEAD_DIM)),
        "kn_gain": gain(ks[5], (DEPTH, HEAD_DIM)),
        "w_branch_a": w(ks[6], (DEPTH, W_DIL, D_MODEL), W_DIL),
        "w_branch_b": w(ks[7], (DEPTH, W_SB, D_MODEL), W_SB),
        "w_out": w(ks[8], (DEPTH, D_MODEL, D_MODEL), D_MODEL),
        "g_mlp": gain(ks[9], (DEPTH, D_MODEL)),
        "w_up": w(ks[10], (DEPTH, D_MODEL, D_FF), D_MODEL),
        "w_down": w(ks[11], (DEPTH, D_FF, D_MODEL), D_FF),
        "g_ple": gain(ks[12], (DEPTH, D_MODEL)),
        "w_ple_gate": w(ks[13], (DEPTH, D_MODEL, D_MODEL), D_MODEL),
        "w_ple_proj": w(ks[14], (DEPTH, PLE_DIM, D_MODEL), PLE_DIM),
    }


def reference(x, p, g_mix, w_in, qn_gain, kn_gain, w_branch_a, w_branch_b, w_out,
              g_mlp, w_up, w_down, g_ple, w_ple_gate, w_ple_proj):
    B, S, _ = x.shape
    pos = jnp.arange(S, dtype=jnp.float32)
    splits = np.cumsum([W_DIL, W_DIL, W_DIL, W_SB, W_SB, W_SB, D_MODEL])
    for i in range(DEPTH):
        h = rmsnorm(x, g_mix[i])
        proj = h @ w_in[i]
        qa, ka, va, qb, kb, vb, ga, gb = jnp.split(proj, splits, axis=-1)
        qa = qa.reshape(B, S, N_HEADS_DIL, HEAD_DIM)
        ka = ka.reshape(B, S, N_HEADS_DIL, HEAD_DIM)
        va = va.reshape(B, S, N_HEADS_DIL, HEAD_DIM)
        qa = partial_rope(rmsnorm(qa, qn_gain[i]), pos)
        ka = partial_rope(rmsnorm(ka, kn_gain[i]), pos)
        ya = dilated_mixture(qa, ka, va).astype(x.dtype).reshape(B, S, W_DIL)

        qb = qb.reshape(B, S, N_HEADS_SB, HEAD_DIM)
        kb = kb.reshape(B, S, N_HEADS_SB, HEAD_DIM)
        vb = vb.reshape(B, S, N_HEADS_SB, HEAD_DIM)
        yb = stick_breaking(qb, kb, vb).astype(x.dtype).reshape(B, S, W_SB)

        merged = jax.nn.sigmoid(ga) * (ya @ w_branch_a[i]) + jax.nn.sigmoid(gb) * (yb @ w_branch_b[i])
        x = x + merged @ w_out[i]

        hm = rmsnorm(x, g_mlp[i])
        x = x + jnp.square(jax.nn.relu(hm @ w_up[i])) @ w_down[i]

        hp = rmsnorm(x, g_ple[i])
        x = x + (p[i] @ w_ple_proj[i]) * jax.nn.sigmoid(hp @ w_ple_gate[i])
    return x
```

```python
import contextlib
import bisect
import numpy as np
import concourse.bass as bass
import concourse.mybir as mybir
from concourse.bass_utils import run_bass_kernel_spmd

F32 = mybir.dt.float32
BF16 = mybir.dt.bfloat16
AF = mybir.ActivationFunctionType
ALU = mybir.AluOpType
AX = mybir.AxisListType

D = 2048
KC = 16
DFF = 8192
HD = 128
NH = 8
PLE = 256
EPS = 1e-6
SCALE = HD ** -0.5
ENG = ["pe", "act", "dve", "pool", "sp"]
KNOB = {"LA_A": 3, "LA_B": 2, "LA_Q": 1, "PRUNE": True}


class T:
    __slots__ = ("name", "h", "last_w", "readers")

    def __init__(self, name, h=None):
        self.name = name
        self.h = h
        self.last_w = None
        self.readers = []

    def __getitem__(self, k):
        return self.h[k]

    def v3(self, a):
        return self.h.rearrange("p (a b) -> p a b", a=a)


class Op:
    __slots__ = ("eng", "fn", "deps", "dma", "semkey", "idx", "needed", "cnt")


class Prog:
    def __init__(self, nc):
        self.nc = nc
        self.ops = []
        self.barriers = []
        self.stack = contextlib.ExitStack()

    def dram(self, name, shape, dt, kind="Internal"):
        return self.nc.dram_tensor(name, list(shape), dt, kind=kind).ap()

    def add(self, eng, fn, reads=(), writes=(), dma=False, semkey=None):
        op = Op()
        op.eng, op.fn, op.dma, op.semkey = eng, fn, dma, semkey
        op.deps = set()
        op.needed = False
        op.cnt = 0
        op.idx = len(self.ops)
        for t in reads:
            if t.last_w is not None:
                op.deps.add(t.last_w)
            if dma or not KNOB['PRUNE']:
                t.readers.append(op)
            else:
                t.readers = [r for r in t.readers if r.dma or r.eng != eng]
                t.readers.append(op)
        for t in writes:
            if t.last_w is not None:
                op.deps.add(t.last_w)
            for r in t.readers:
                if r is not op:
                    op.deps.add(r)
            t.last_w = op
            t.readers = []
        assert not (dma and semkey is None)
        self.ops.append(op)
        return op

    def barrier(self):
        self.barriers.append(len(self.ops))

    def emit(self):
        nc = self.nc
        ops = self.ops
        def phase_of(idx):
            return bisect.bisect_right(self.barriers, idx)
        for op in ops:
            if op.eng == "pe" and not op.dma:
                op.deps = {d for d in op.deps if not (d.eng == "pe" and not d.dma)}
            op.deps.discard(op)
            ph = phase_of(op.idx)
            op.deps = {d for d in op.deps if phase_of(d.idx) == ph}
            for d in op.deps:
                d.needed = True
        bar_last = []
        for pos in self.barriers:
            last = {}
            for op in ops[:pos][::-1]:
                if not op.dma and op.eng not in last:
                    last[op.eng] = op
                    if len(last) == 4:
                        break
            bar_last.append(last)
        for last in bar_last:
            for op in last.values():
                op.needed = True
        engsem = {en: self.stack.enter_context(nc.semaphore("s_" + en)) for en in ENG}
        phys = []
        physcount = []
        physcls = []
        keymap = {}
        cnt = {en: 0 for en in ENG}
        per_sem = {}
        bpos = list(self.barriers)
        nbp = 0
        for op in ops:
            while nbp < len(bpos) and bpos[nbp] <= op.idx:
                keymap = {}
                nbp += 1
            if op.dma:
                k = id(op.semkey)
                if k not in keymap:
                    cls = "sw" if op.eng == "pool" else "hw"
                    used = [v for v in keymap.values() if physcls[v] == cls]
                    free = [pi_ for pi_ in range(len(phys)) if physcls[pi_] == cls and pi_ not in used]
                    if free:
                        pi = free[0]
                    else:
                        pi = len(phys)
                        phys.append(self.stack.enter_context(nc.semaphore("d%s_%d" % (cls, pi))))
                        physcount.append(0)
                        physcls.append(cls)
                    keymap[k] = pi
                pi = keymap[k]
                physcount[pi] += 16
                op.cnt = physcount[pi]
                op.semkey = pi
                per_sem.setdefault(pi, []).append((op.idx, op.cnt))
            elif op.needed:
                cnt[op.eng] += 1
                op.cnt = cnt[op.eng]
        self.n_sems = len(phys) + 5
        self.max_counts = (dict(cnt), list(physcount))
        dmasem = {pi: phys[pi] for pi in range(len(phys))}
        dma_order = list(range(len(phys)))
        per_sem_idx = {k: [a for a, _ in v] for k, v in per_sem.items()}
        bar_waits = []
        for bi, pos in enumerate(self.barriers):
            wl = []
            for en, op in bar_last[bi].items():
                wl.append((("e", en), engsem[en], op.cnt))
            for k in dma_order:
                lst = per_sem_idx[k]
                j = bisect.bisect_left(lst, pos) - 1
                if j >= 0:
                    wl.append((("d", k), dmasem[k], per_sem[k][j][1]))
            bar_waits.append(wl)
        streams = {en: [] for en in ENG}
        for op in ops:
            streams[op.eng].append(op)
        nbar = len(self.barriers)

        def run_stream(en, e):
            waited = {}
            nb = 0

            def do_wait(key, sem, c):
                if waited.get(key, 0) < c:
                    e.wait_ge(sem, c)
                    waited[key] = c

            for op in streams[en]:
                while nb < nbar and self.barriers[nb] <= op.idx:
                    for key, sem, c in bar_waits[nb]:
                        do_wait(key, sem, c)
                    nb += 1
                need = {}
                for d in op.deps:
                    if d.dma:
                        k = d.semkey
                        lst = per_sem_idx[k]
                        j = bisect.bisect_left(lst, op.idx) - 1
                        c = per_sem[k][j][1]
                        key, sem = ("d", k), dmasem[k]
                    else:
                        c = d.cnt
                        key, sem = ("e", d.eng), engsem[d.eng]
                    if need.get(key, (None, 0))[1] < c:
                        need[key] = (sem, c)
                for key, (sem, c) in need.items():
                    do_wait(key, sem, c)
                ins = op.fn(e)
                if op.dma:
                    ins.then_inc(dmasem[op.semkey], 16)
                elif op.needed:
                    ins.then_inc(engsem[en], 1)
            while nb < nbar:
                for key, sem, c in bar_waits[nb]:
                    do_wait(key, sem, c)
                nb += 1

        with nc.Block() as block:
            @block.tensor
            def _(e):
                run_stream("pe", e)

            @block.scalar
            def _(e):
                run_stream("act", e)

            @block.vector
            def _(e):
                run_stream("dve", e)

            @block.gpsimd
            def _(e):
                run_stream("pool", e)

            @block.sync
            def _(e):
                run_stream("sp", e)
        self.stack.close()


class Arena:
    def __init__(self, P, nbytes):
        self.P = P
        self.n = nbytes // 2
        self.h = P.stack.enter_context(P.nc.sbuf_tensor("arena", [128, self.n], BF16))
        self.base = 0
        self.off = 0
        self.top = self.n

    def freeze(self):
        self.base = self.off

    def top_bf(self, name, n):
        n2 = (n + 15) // 16 * 16
        self.top -= n2
        assert self.top >= self.off, ("arena top overflow", name)
        return T(name, self.h[:, self.top:self.top + n])

    def release_top(self):
        self.top = self.n

    def reset(self):
        self.off = self.base

    def bf(self, name, n):
        n2 = (n + 15) // 16 * 16
        assert self.off + n2 <= self.top, ("arena overflow", name, self.off, n2, self.top)
        t = T(name, self.h[:, self.off:self.off + n])
        self.off += n2
        return t

    def f32(self, name, n):
        m = (2 * n + 15) // 16 * 16
        assert self.off + m <= self.top, ("arena overflow", name, self.off, m, self.top)
        t = T(name, self.h[:, self.off:self.off + 2 * n].bitcast(F32))
        self.off += m
        return t


class Prefetch:
    def __init__(self, slots, n, dist, issue):
        assert dist < len(slots)
        self.slots, self.n, self.dist, self.issue = slots, n, dist, issue
        self.next = 0

    def get(self, k):
        while self.next < self.n and self.next <= k + self.dist:
            self.issue(self.next, self.slots[self.next % len(self.slots)])
            self.next += 1
        return self.slots[k % len(self.slots)]

def op_dma(P, eng, out_ap, in_ap, reads, writes, semkey):
    P.add(eng, lambda e: e.dma_start(out=out_ap, in_=in_ap), reads=reads, writes=writes, dma=True, semkey=semkey)


def op_mm(P, out_ap, lhsT, rhs, start, stop, reads, writes, skip=False):
    if skip:
        P.add("pe", lambda e: e.matmul(out_ap, lhsT=lhsT, rhs=rhs, start=start, stop=stop, skip_group_check=True),
              reads=reads, writes=writes)
    else:
        P.add("pe", lambda e: e.matmul(out_ap, lhsT=lhsT, rhs=rhs, start=start, stop=stop), reads=reads, writes=writes)


def op_tr(P, out_ap, in_ap, ident_ap, reads, writes):
    P.add("pe", lambda e: e.transpose(out=out_ap, in_=in_ap, identity=ident_ap), reads=reads, writes=writes)


def op_act(P, out_ap, in_ap, func, reads, writes, scale=1.0, bias=0.0, accum=None):
    if accum is None:
        P.add("act", lambda e: e.activation(out=out_ap, in_=in_ap, func=func, scale=scale, bias=bias), reads=reads, writes=writes)
    else:
        P.add("act", lambda e: e.activation(out=out_ap, in_=in_ap, func=func, scale=scale, bias=bias, accum_out=accum),
              reads=reads, writes=writes)


def op_tt(P, eng, out_ap, a_ap, b_ap, op, reads, writes):
    P.add(eng, lambda e: e.tensor_tensor(out=out_ap, in0=a_ap, in1=b_ap, op=op), reads=reads, writes=writes)


def op_copy(P, eng, out_ap, in_ap, reads, writes):
    if eng == "act":
        P.add("act", lambda e: e.activation(out=out_ap, in_=in_ap, func=AF.Copy), reads=reads, writes=writes)
    else:
        P.add(eng, lambda e: e.tensor_copy(out=out_ap, in_=in_ap), reads=reads, writes=writes)


def op_stt(P, eng, out_ap, in0, scalar, in1, op0, op1, reads, writes):
    P.add(eng, lambda e: e.scalar_tensor_tensor(out=out_ap, in0=in0, scalar=scalar, in1=in1, op0=op0, op1=op1),
          reads=reads, writes=writes)


def op_memset(P, eng, ap, val, writes):
    P.add(eng, lambda e: e.memset(ap, val), writes=writes)


def build(NSLOT=4, debug=False):
    WSP = 4 * NSLOT
    SW = 512 * WSP
    NOWN = 512 * NSLOT
    NBW = SW // 128
    own_pos = [4 * i + 3 for i in range(NSLOT)]

    nc = bass.Bass("TRN2", target_bir_lowering=False)
    P = Prog(nc)
    IN, OUT = "ExternalInput", "ExternalOutput"
    SCR = OUT if debug else "Internal"
    xw = P.dram("xw", [SW, D], F32, IN)
    p_own = P.dram("p_own", [NOWN, PLE], F32, IN)
    g_mix = P.dram("g_mix", [1, D], F32, IN)
    g_mlp = P.dram("g_mlp", [1, D], F32, IN)
    g_ple = P.dram("g_ple", [1, D], F32, IN)
    qn_g = P.dram("qn_gain", [1, HD], F32, IN)
    kn_g = P.dram("kn_gain", [1, HD], F32, IN)
    w_in = P.dram("w_in", [D, 10240], F32, IN)
    w_a = P.dram("w_branch_a", [1024, D], F32, IN)
    w_b = P.dram("w_branch_b", [1024, D], F32, IN)
    w_out = P.dram("w_out", [D, D], F32, IN)
    w_up = P.dram("w_up", [D, DFF], F32, IN)
    w_down = P.dram("w_down", [DFF, D], F32, IN)
    w_pg = P.dram("w_ple_gate", [D, D], F32, IN)
    w_pp = P.dram("w_ple_proj", [PLE, D], F32, IN)
    cs_d = P.dram("cs_tab", [SW, 32], F32, IN)
    c_ident = P.dram("c_ident", [128, 128], F32, IN)
    c_tri = P.dram("c_tri", [128, 128], F32, IN)
    c_slow = P.dram("c_slow", [128, 128], F32, IN)
    c_valid = P.dram("c_valid", [128, 12 * 128], F32, IN)
    c_mall = P.dram("c_mall", [128, 23 * 128], F32, IN)
    c_msb = P.dram("c_msb", [128, 4 * 512], F32, IN)
    out_own = P.dram("out_own", [NOWN, D], F32, OUT)

    hT_d = P.dram("hT_d", [KC, 128, SW], BF16, SCR)
    KAT_d = P.dram("KAT_d", [NH, 128, SW], BF16, SCR)
    KBT_d = P.dram("KBT_d", [NH, 128, SW], BF16, SCR)
    VA_d = P.dram("VA_d", [NH, 128, NBW, 128], BF16, SCR)
    VB_d = P.dram("VB_d", [NH, 128, NBW, 128], BF16, SCR)
    QAT_d = P.dram("QAT_d", [NH, 128, NOWN], BF16, SCR)
    QBT_d = P.dram("QBT_d", [NH, 128, NOWN], BF16, SCR)
    SG_d = P.dram("SG_d", [2, KC, 128, NOWN], BF16, SCR)
    YT_d = P.dram("YT_d", [2, NH, 128, NOWN], BF16, SCR)
    MT_d = P.dram("MT_d", [KC, 128, NOWN], BF16, SCR)
    X1_d = P.dram("X1_d", [NOWN, D], F32, SCR)
    hmT_d = P.dram("hmT_d", [KC, 128, NOWN], BF16, SCR)
    X2_d = P.dram("X2_d", [NOWN, D], F32, SCR)
    hpT_d = P.dram("hpT_d", [KC, 128, NOWN], BF16, SCR)
    pT_d = P.dram("pT_d", [2, 128, NOWN], BF16, SCR)

    A = Arena(P, 200 * 1024)
    banks = [T("bank%d" % i, P.stack.enter_context(nc.psum_tensor("bank%d" % i, [128, 512], F32))) for i in range(8)]

    def bank_bf(b):
        return b.h[:, :].bitcast(BF16)

    ident = A.bf("ident", 128)
    tri = A.bf("tri", 128)
    slow = A.bf("slow", 128)
    ones = A.bf("ones", 128)
    valid = A.bf("valid", 12 * 128)
    mall = A.bf("mall", 23 * 128)
    msb = A.bf("msb", 4 * 512)
    A.freeze()
    for t, src in ((ident, c_ident), (tri, c_tri), (slow, c_slow), (valid, c_valid), (mall, c_mall), (msb, c_msb)):
        op_dma(P, "pool", t.h, src, [], [t], t)
    op_memset(P, "dve", ones.h, 1.0, [ones])

    def load_w(t, w_ap, c0, ncols, kcn):
        op_dma(P, "pool", t.h[:, 0:kcn * ncols].rearrange("p (a b) -> p a b", a=kcn),
               w_ap[:, c0:c0 + ncols].rearrange("(kc p) n -> p kc n", p=128), [], [t], t)

    def load_at(t, srcT, t0, kcn):
        op_dma(P, "sp", t.h[:, 0:kcn * 512].rearrange("p (a b) -> p a b", a=kcn),
               srcT[:, :, t0:t0 + 512].rearrange("k p t -> p k t"), [], [t], t)

    def normT(src_rows, nblk, gain_ap, dstT, pre=None):
        A.reset()
        if pre is not None:
            pre()
        gbc = A.f32("gbc", D)
        xs3 = [A.f32("xs%d" % i, D) for i in range(3)]
        junk = A.bf("junk", D)
        ssq = [A.f32("ssq%d" % i, 8) for i in range(2)]
        hn = [A.bf("hn%d" % i, D) for i in range(2)]
        stage = [A.bf("stg%d" % i, KC * 512) for i in range(2)]
        op_dma(P, "sp", gbc.h, gain_ap.broadcast_to([128, D]), [], [gbc], gbc)
        xpf = Prefetch(xs3, nblk, 2, lambda k, t_: op_dma(P, "sp", t_.h, src_rows(k), [], [t_], t_))

        def nA(b):
            s = b % 2
            x_t = xpf.get(b)
            op_memset(P, "pool", ssq[s].h, 0.0, [ssq[s]])
            op_act(P, junk.h, x_t.h, AF.Square, [x_t], [junk, ssq[s]], accum=ssq[s][:, 0:1])
            op_act(P, ssq[s][:, 1:2], ssq[s][:, 0:1], AF.Ln, [ssq[s]], [ssq[s]], scale=1.0 / D, bias=EPS)
            op_act(P, ssq[s][:, 2:3], ssq[s][:, 1:2], AF.Exp, [ssq[s]], [ssq[s]], scale=-0.5)
            op_stt(P, "dve", hn[s].h, x_t.h, ssq[s][:, 2:3], gbc.h, ALU.mult, ALU.mult, [x_t, ssq[s], gbc], [hn[s]])

        def nB(b):
            s = b % 2
            t, bi = b // 4, b % 4
            st = stage[t % 2]
            for half in range(2):
                bk = banks[(2 * b + half) % 4]
                bb = bank_bf(bk)
                for k8 in range(8):
                    kc = half * 8 + k8
                    op_tr(P, bb[:, k8 * 128:(k8 + 1) * 128], hn[s][:, kc * 128:(kc + 1) * 128], ident.h, [hn[s], ident], [bk])
                dst = st.v3(KC)[:, half * 8:(half + 1) * 8, bi * 128:(bi + 1) * 128]
                op_copy(P, "act" if half == 0 else "dve", dst, bb.rearrange("p (a b) -> p a b", a=8), [bk], [st])
            if bi == 3 or b == nblk - 1:
                op_dma(P, "pool", dstT[:, :, t * 512:(t + 1) * 512].rearrange("k p t -> p k t"), st.v3(KC), [st], [], st)

        nA(0)
        for b in range(nblk):
            if b + 1 < nblk:
                nA(b + 1)
            nB(b)
        P.barrier()

    pre_w = {}

    def pre_phase1():
        pre_w["wt"] = [A.top_bf("wt%d" % i, KC * 1024) for i in range(2)]
        load_w(pre_w["wt"][0], w_in, 1024, 1024, KC)
        load_w(pre_w["wt"][1], w_in, 2048, 1024, KC)

    normT(lambda b: xw[b * 128:(b + 1) * 128, :], NBW, g_mix, hT_d, pre=pre_phase1)

    A.reset()
    wt = pre_w["wt"]
    at = [A.bf("at%d" % i, KC * 512) for i in range(3)]
    stg = [A.bf("pstg%d" % i, 8 * 512) for i in range(2)]
    sqs = [A.f32("sq%d" % i, 1024) for i in range(2)]
    kns = [A.f32("kn%d" % i, 1024) for i in range(2)]
    knb = [A.bf("knb%d" % i, 1024) for i in range(2)]
    s8 = [A.f32("s8_%d" % i, 32) for i in range(2)]
    cst = [A.f32("cs%d" % i, 32) for i in range(2)]
    rt = [A.f32("rt%d" % i, 128) for i in range(8)]
    qgb = A.f32("qgb", 128)
    kgb = A.f32("kgb", 128)
    op_dma(P, "sp", qgb.h, qn_g.broadcast_to([128, HD]), [], [qgb], qgb)
    op_dma(P, "sp", kgb.h, kn_g.broadcast_to([128, HD]), [], [kgb], kgb)

    all_tiles = list(range(WSP))
    own_tiles = own_pos
    passes = [
        ("ka", 1024, all_tiles, "qk", (kgb, KAT_d, False)),
        ("va", 2048, all_tiles, "v", VA_d),
        ("kb", 4096, all_tiles, "fm", (KBT_d, False)),
        ("vb", 5120, all_tiles, "v", VB_d),
        ("qa", 0, own_tiles, "qk", (qgb, QAT_d, True)),
        ("qb", 3072, own_tiles, "fm", (QBT_d, True)),
        ("ga0", 6144, own_tiles, "sg", (0, 0)),
        ("ga1", 7168, own_tiles, "sg", (0, 1)),
        ("gb0", 8192, own_tiles, "sg", (1, 0)),
        ("gb1", 9216, own_tiles, "sg", (1, 1)),
    ]

    cnt = {"at": 0, "bank": 0, "stg": 0, "blk": 0}

    def next_bank():
        b = banks[cnt["bank"] % 4]
        cnt["bank"] += 1
        return b

    def qk_pass(w, wv, tiles, arg):
        gb_t, dstT, is_own = arg
        blocks = [(ti, tpos, bi) for ti, tpos in enumerate(tiles) for bi in range(4)]
        tinfo = {}

        def qkA(idx):
            ti, tpos, bi = blocks[idx]
            if bi == 0:
                a = atpf.get(item_index[(cur["pi"], ti)])
                st = stg[cnt["stg"] % 2]
                cnt["stg"] += 1
                tinfo[ti] = (a, st)
            a, st = tinfo[ti]
            av = a.v3(KC)
            bb = cnt["blk"]
            cnt["blk"] += 1
            s8t, cs, kb_t = s8[bb % 2], cst[bb % 2], knb[bb % 2]
            sq_t, kn_t = sqs[bb % 2], kns[bb % 2]
            rts = rt[4 * (bb % 2):4 * (bb % 2) + 4]
            r0 = tpos * 512 + bi * 128
            op_dma(P, "sp", cs.h, cs_d[r0:r0 + 128, :], [], [cs], cs)
            bks = []
            for n2 in range(2):
                bk = next_bank()
                bks.append(bk)
                for kc in range(KC):
                    op_mm(P, bk.h[:, :], av[:, kc, bi * 128:(bi + 1) * 128], wv[:, kc, n2 * 512:(n2 + 1) * 512],
                          kc == 0, kc == KC - 1, [w, a], [bk])
                op_act(P, sq_t[:, n2 * 512:(n2 + 1) * 512], bk.h[:, :], AF.Square, [bk], [sq_t])
            P.add("dve", (lambda o, i: (lambda e: e.tensor_reduce(out=o, in_=i, axis=AX.X, op=ALU.add)))(
                s8t[:, 0:8], sq_t.v3(8)), reads=[sq_t], writes=[s8t])
            op_act(P, s8t[:, 8:16], s8t[:, 0:8], AF.Ln, [s8t], [s8t], scale=1.0 / HD, bias=EPS)
            op_act(P, s8t[:, 16:24], s8t[:, 8:16], AF.Exp, [s8t], [s8t], scale=-0.5)
            for n2 in range(2):
                op_tt(P, "dve", kn_t.h[:, n2 * 512:(n2 + 1) * 512].rearrange("p (a b) -> p a b", a=4),
                      bks[n2].h[:, :].rearrange("p (a b) -> p a b", a=4),
                      s8t[:, 16 + n2 * 4:16 + (n2 + 1) * 4].unsqueeze(2).broadcast_to([128, 4, 128]),
                      ALU.mult, [bks[n2], s8t], [kn_t])
            k3 = kn_t.v3(8)
            op_tt(P, "pool", k3, k3, gb_t.h.unsqueeze(1).broadcast_to([128, 8, 128]), ALU.mult, [kn_t, gb_t], [kn_t])
            kb3 = kb_t.v3(8)
            op_copy(P, "dve", kb3[:, :, 32:128], k3[:, :, 32:128], [kn_t], [kb_t])
            cosb = cs[:, 0:16].unsqueeze(1).broadcast_to([128, 8, 16])
            sinb = cs[:, 16:32].unsqueeze(1).broadcast_to([128, 8, 16])
            x1, x2 = k3[:, :, 0:16], k3[:, :, 16:32]
            r = [t_.v3(8) for t_ in rts]
            op_tt(P, "pool", r[0], x1, cosb, ALU.mult, [kn_t, cs], [rts[0]])
            op_tt(P, "pool", r[1], x2, sinb, ALU.mult, [kn_t, cs], [rts[1]])
            op_tt(P, "pool", r[2], x2, cosb, ALU.mult, [kn_t, cs], [rts[2]])
            op_tt(P, "pool", r[3], x1, sinb, ALU.mult, [kn_t, cs], [rts[3]])
            op_tt(P, "pool", kb3[:, :, 0:16], r[0], r[1], ALU.subtract, [rts[0], rts[1]], [kb_t])
            op_tt(P, "pool", kb3[:, :, 16:32], r[2], r[3], ALU.add, [rts[2], rts[3]], [kb_t])
            tinfo[("kb", idx)] = (kb_t, bb)

        def qkB(idx):
            ti, tpos, bi = blocks[idx]
            a, st = tinfo[ti]
            kb_t, bb = tinfo.pop(("kb", idx))
            kb3 = kb_t.v3(8)
            tb = banks[4 + bb % 2]
            tbb = bank_bf(tb)
            for h in range(NH):
                op_tr(P, tbb[:, h * 128:(h + 1) * 128], kb3[:, h, :], ident.h, [kb_t, ident], [tb])
            op_copy(P, "act" if bb % 2 == 0 else "dve", st.v3(8)[:, :, bi * 128:(bi + 1) * 128],
                    tbb.rearrange("p (a b) -> p a b", a=8), [tb], [st])
            if bi == 3:
                t0 = ti * 512 if is_own else tpos * 512
                op_dma(P, "sp", dstT[:, :, t0:t0 + 512].rearrange("h d t -> d h t"), st.v3(8), [st], [], st)

        nb_ = len(blocks)
        LQ = KNOB['LA_Q']
        for idx in range(min(LQ, nb_)):
            qkA(idx)
        for idx in range(nb_):
            if idx + LQ < nb_:
                qkA(idx + LQ)
            qkB(idx)

    items = [(pi_, ti_, tp_) for pi_, ps_ in enumerate(passes) for ti_, tp_ in enumerate(ps_[2])]
    item_index = {(pi_, ti_): k for k, (pi_, ti_, tp_) in enumerate(items)}
    atpf = Prefetch(at, len(items), 2, lambda k, t_: load_at(t_, hT_d, items[k][2] * 512, KC))
    cur = {"pi": 0}
    assert passes[0][1] == 1024 and passes[1][1] == 2048
    for pi, (pname, c0, tiles, mode, arg) in enumerate(passes):
        cur["pi"] = pi
        w = wt[pi % 2]
        if pi >= 1 and pi + 1 < len(passes):
            load_w(wt[(pi + 1) % 2], w_in, passes[pi + 1][1], 1024, KC)
        wv = w.v3(KC)
        if mode == "qk":
            qk_pass(w, wv, tiles, arg)
            continue
        for ti, tpos in enumerate(tiles):
            a = atpf.get(item_index[(pi, ti)])
            av = a.v3(KC)
            own_t0 = ti * 512
            st = stg[cnt["stg"] % 2]
            cnt["stg"] += 1
            if mode in ("fm", "sg"):
                for n8 in range(8):
                    bk = next_bank()
                    for kc in range(KC):
                        op_mm(P, bk.h[:, :], wv[:, kc, n8 * 128:(n8 + 1) * 128], av[:, kc, :], kc == 0, kc == KC - 1, [w, a], [bk])
                    dst = st.v3(8)[:, n8, :]
                    if mode == "fm":
                        op_copy(P, "act" if n8 % 2 == 0 else "dve", dst, bk.h[:, :], [bk], [st])
                    else:
                        op_act(P, dst, bk.h[:, :], AF.Sigmoid, [bk], [st])
                if mode == "fm":
                    dstT, is_own = arg
                    t0 = own_t0 if is_own else tpos * 512
                    op_dma(P, "sp", dstT[:, :, t0:t0 + 512].rearrange("h d t -> d h t"), st.v3(8), [st], [], st)
                else:
                    ab, half = arg
                    op_dma(P, "sp", SG_d[ab, half * 8:(half + 1) * 8, :, own_t0:own_t0 + 512].rearrange("k p t -> p k t"),
                           st.v3(8), [st], [], st)
            elif mode == "v":
                for bi in range(4):
                    for n2 in range(2):
                        bk = next_bank()
                        for kc in range(KC):
                            op_mm(P, bk.h[:, :], av[:, kc, bi * 128:(bi + 1) * 128], wv[:, kc, n2 * 512:(n2 + 1) * 512],
                                  kc == 0, kc == KC - 1, [w, a], [bk])
                        dst = st.h.rearrange("p (h b d) -> p h b d", h=NH, b=4)[:, n2 * 4:(n2 + 1) * 4, bi, :]
                        op_copy(P, "act" if n2 == 0 else "dve", dst, bk.h[:, :].rearrange("p (h d) -> p h d", h=4), [bk], [st])
                blk0 = tpos * 4
                op_dma(P, "sp", arg[:, :, blk0:blk0 + 4, :].rearrange("h p b d -> p h (b d)"),
                       st.h.rearrange("p (h x) -> p h x", h=NH), [st], [], st)
    A.release_top()
    P.barrier()

    A.reset()
    qt = [A.bf("qt%d" % i, 512) for i in range(2)]
    ktA = [A.bf("ktA%d" % i, 2560) for i in range(2)]
    vtA = [A.bf("vtA%d" % i, 20 * 128) for i in range(2)]
    ef = [A.bf("ef%d" % i, 512) for i in range(4)]
    ab = [A.bf("ab%d" % i, 512) for i in range(4)]
    rden = [A.f32("rden%d" % i, 512) for i in range(2)]
    yst = [A.bf("yst%d" % i, 512) for i in range(2)]
    it = 0
    gstep = [0]
    LA_A = KNOB['LA_A']

    def issue_a(k, s_):
        i_, h_ = k // NH, k % NH
        qb0_ = 4 * own_pos[i_]
        lo_ = max(0, qb0_ - 16)
        hi_ = qb0_ + 3
        n_ = hi_ - lo_ + 1
        op_dma(P, "sp", qt[s_].h, QAT_d[h_, :, i_ * 512:(i_ + 1) * 512], [], [qt[s_]], qt[s_])
        op_dma(P, "sp", ktA[s_].h[:, 0:n_ * 128], KAT_d[h_, :, lo_ * 128:(hi_ + 1) * 128], [], [ktA[s_]], ktA[s_])
        op_dma(P, "sp", vtA[s_].h[:, 0:n_ * 128].rearrange("p (b d) -> p b d", d=128), VA_d[h_, :, lo_:hi_ + 1, :],
               [], [vtA[s_]], vtA[s_])

    apf = Prefetch([0, 1], NSLOT * NH, 1, issue_a)
    for i in range(NSLOT):
        qb0 = 4 * own_pos[i]
        kb_lo = max(0, qb0 - 16)
        kb_hi = qb0 + 3
        nkb = kb_hi - kb_lo + 1
        for h in range(NH):
            s = it % 2
            apf.get(it)
            it += 1
            py = banks[4 + s]
            pd = banks[6 + s]
            g0 = gstep[0]

            def stA(m):
                step = g0 + m
                kabs = kb_lo + m
                ps = banks[step % 4]
                e_t = ef[step % 4]
                a_t = ab[step % 4]
                op_mm(P, ps.h[:, :], ktA[s][:, m * 128:(m + 1) * 128], qt[s].h, True, True, [ktA[s], qt[s]], [ps])
                op_act(P, e_t.h, ps.h[:, :], AF.Exp, [ps], [e_t], scale=SCALE)
                k0 = qb0 - kabs
                msl = mall.h[:, (k0 + 3) * 128:(k0 + 7) * 128]
                op_tt(P, "dve" if step % 4 != 3 else "pool", a_t.h, e_t.h, msl, ALU.mult, [e_t, mall], [a_t])

            def stB(m):
                step = g0 + m
                kabs = kb_lo + m
                a_t = ab[step % 4]
                op_mm(P, py.h[:, :], vtA[s][:, m * 128:(m + 1) * 128], a_t.h, m == 0, m == nkb - 1, [vtA[s], a_t], [py])
                if kabs < 12:
                    dl, dl_t = valid.h[:, kabs * 128:(kabs + 1) * 128], valid
                else:
                    dl, dl_t = ones.h, ones
                op_mm(P, pd.h[:, :], dl, a_t.h, m == 0, m == nkb - 1, [dl_t, a_t], [pd])

            for m in range(min(LA_A, nkb)):
                stA(m)
            for m in range(nkb):
                if m + LA_A < nkb:
                    stA(m + LA_A)
                stB(m)
            gstep[0] += nkb
            rd = rden[s]
            P.add("dve", (lambda o, i_: (lambda e: e.reciprocal(out=o, in_=i_)))(rd.h, pd.h[:, :]), reads=[pd], writes=[rd])
            op_tt(P, "dve", yst[s].h, py.h[:, :], rd.h, ALU.mult, [py, rd], [yst[s]])
            op_dma(P, "sp", YT_d[0, h, :, i * 512:(i + 1) * 512], yst[s].h, [yst[s]], [], yst[s])
    P.barrier()

    A.reset()
    NKMAX = 4 * (own_pos[-1] + 1)
    qtb = [[A.bf("qtb%d%d" % (u, i), 512) for i in range(2)] for u in range(2)]
    ktB = [[A.bf("ktB%d%d" % (u, i), NKMAX * 128) for i in range(2)] for u in range(2)]
    vtB = [[A.bf("vtB%d%d" % (u, i), NKMAX * 128) for i in range(2)] for u in range(2)]
    LA_B = KNOB['LA_B']
    NB_E = LA_B + 3
    e1 = [[A.f32("e1_%d%d" % (u, i), 512) for i in range(NB_E)] for u in range(2)]
    lp = [[A.bf("lp%d%d" % (u, i), 512) for i in range(NB_E)] for u in range(2)]
    wf = [[A.f32("wf%d%d" % (u, i), 512) for i in range(2)] for u in range(2)]
    aB = [[A.bf("aB%d%d" % (u, i), 512) for i in range(3)] for u in range(2)]
    ysb = [[A.bf("ysb%d%d" % (u, i), 512) for i in range(2)] for u in range(2)]
    itb = 0

    def issue_b(k, sl_):
        i_, hp_ = k // (NH // 2), k % (NH // 2)
        nk_ = 4 * (own_pos[i_] + 1)
        for u in range(2):
            h_ = 2 * hp_ + u
            op_dma(P, "sp", qtb[u][sl_].h, QBT_d[h_, :, i_ * 512:(i_ + 1) * 512], [], [qtb[u][sl_]], qtb[u][sl_])
            op_dma(P, "sp", ktB[u][sl_].h[:, 0:nk_ * 128], KBT_d[h_, :, 0:nk_ * 128], [], [ktB[u][sl_]], ktB[u][sl_])
            op_dma(P, "sp", vtB[u][sl_].h[:, 0:nk_ * 128].rearrange("p (b d) -> p b d", d=128), VB_d[h_, :, 0:nk_, :],
                   [], [vtB[u][sl_]], vtB[u][sl_])

    bpf = Prefetch([0, 1], NSLOT * (NH // 2), 1, issue_b)
    for i in range(NSLOT):
        nk = 4 * (own_pos[i] + 1)
        for hp in range(NH // 2):
            sl = itb % 2
            bpf.get(itb)
            itb += 1
            def sbA(st_i):
                m = nk - 1 - st_i
                mrel = m - (nk - 4)
                for u in range(2):
                    q_t, k_t = qtb[u][sl], ktB[u][sl]
                    pz = banks[4 * u + st_i % 2]
                    op_mm(P, pz.h[:, :], k_t[:, m * 128:(m + 1) * 128], q_t.h, True, True, [k_t, q_t], [pz])
                for u in range(2):
                    pz = banks[4 * u + st_i % 2]
                    e_t = e1[u][st_i % NB_E]
                    op_act(P, e_t.h, pz.h[:, :], AF.Exp, [pz], [e_t], scale=SCALE)
                    if mrel >= 0:
                        op_tt(P, "dve", e_t.h, e_t.h, msb.h[:, mrel * 512:(mrel + 1) * 512], ALU.mult, [e_t, msb], [e_t])
                for u in range(2):
                    e_t = e1[u][st_i % NB_E]
                    l_t = lp[u][st_i % NB_E]
                    op_act(P, l_t.h, e_t.h, AF.Ln, [e_t], [l_t], bias=1.0)

            def sbB(st_i):
                for u in range(2):
                    pR = banks[4 * u + 2]
                    l_t = lp[u][st_i % NB_E]
                    l_prev = lp[u][(st_i - 1) % NB_E]
                    if st_i > 0:
                        op_mm(P, pR.h[:, :], slow.h, l_prev.h, False, False, [slow, l_prev], [pR], skip=True)
                    op_mm(P, pR.h[:, :], tri.h, l_t.h, st_i == 0, (st_i == nk - 1) or KNOB.get('RSTOP', True), [tri, l_t], [pR], skip=True)
                for u in range(2):
                    pR = banks[4 * u + 2]
                    e_t = e1[u][st_i % NB_E]
                    w_t = wf[u][st_i % 2]
                    a_t = aB[u][st_i % 3]
                    op_act(P, w_t.h, pR.h[:, :], AF.Exp, [pR], [w_t], scale=-1.0)
                    op_tt(P, "pool" if u == 0 else "dve", a_t.h, e_t.h, w_t.h, ALU.mult, [e_t, w_t], [a_t])

            def sbC(st_i):
                m = nk - 1 - st_i
                for u in range(2):
                    v_t = vtB[u][sl]
                    pY = banks[4 * u + 3]
                    a_t = aB[u][st_i % 3]
                    op_mm(P, pY.h[:, :], v_t[:, m * 128:(m + 1) * 128], a_t.h, st_i == 0, st_i == nk - 1, [v_t, a_t], [pY])

            for st_i in range(min(LA_B, nk)):
                sbA(st_i)
            for st_i in range(nk):
                if st_i + LA_B < nk:
                    sbA(st_i + LA_B)
                sbB(st_i)
                if st_i >= 1:
                    sbC(st_i - 1)
            sbC(nk - 1)
            for u in range(2):
                h = 2 * hp + u
                op_copy(P, "dve", ysb[u][sl].h, banks[4 * u + 3].h[:, :], [banks[4 * u + 3]], [ysb[u][sl]])
                op_dma(P, "sp", YT_d[1, h, :, i * 512:(i + 1) * 512], ysb[u][sl].h, [ysb[u][sl]], [], ysb[u][sl])
    P.barrier()

    A.reset()
    NT = NOWN // 512
    wab = [[A.bf("wab%d%d" % (x, i), 8 * 1024) for i in range(2)] for x in range(2)]
    yt = [[A.bf("yt%d%d" % (x, i), 8 * 512) for i in range(2)] for x in range(2)]
    sgt = [[A.bf("sgt%d%d" % (x, i), 8 * 512) for i in range(2)] for x in range(2)]
    t1 = [A.f32("t1_%d" % i, 512) for i in range(2)]
    t2 = [A.f32("t2_%d" % i, 512) for i in range(2)]
    mst = [A.bf("mst%d" % i, 8 * 512) for i in range(2)]
    it3 = 0

    def issue_c(k, s_):
        g_, ti_ = k // NT, k % NT
        for x in range(2):
            op_dma(P, "sp", yt[x][s_].v3(8), YT_d[x, :, :, ti_ * 512:(ti_ + 1) * 512].rearrange("k p t -> p k t"),
                   [], [yt[x][s_]], yt[x][s_])
            op_dma(P, "sp", sgt[x][s_].v3(8),
                   SG_d[x, g_ * 8:(g_ + 1) * 8, :, ti_ * 512:(ti_ + 1) * 512].rearrange("k p t -> p k t"),
                   [], [sgt[x][s_]], sgt[x][s_])

    cpf = Prefetch([0, 1], 2 * NT, 1, issue_c)
    for x, wsrc in ((0, w_a), (1, w_b)):
        load_w(wab[x][0], wsrc, 0, 1024, 8)
        load_w(wab[x][1], wsrc, 1024, 1024, 8)
    for g in range(2):
        for x, wsrc in ():
            load_w(wab[x][g % 2], wsrc, g * 1024, 1024, 8)
        for ti in range(NT):
            s = it3 % 2
            cpf.get(it3)
            it3 += 1
            for n8 in range(8):
                pa = banks[(2 * n8) % 4]
                pb = banks[(2 * n8 + 1) % 4]
                for x, pbk in ((0, pa), (1, pb)):
                    wv = wab[x][g % 2].v3(8)
                    yv = yt[x][s].v3(8)
                    for kc in range(8):
                        op_mm(P, pbk.h[:, :], wv[:, kc, n8 * 128:(n8 + 1) * 128], yv[:, kc, :], kc == 0, kc == 7,
                              [wab[x][g % 2], yt[x][s]], [pbk])
                a1, a2 = t1[n8 % 2], t2[n8 % 2]
                op_tt(P, "dve", a1.h, pa.h[:, :], sgt[0][s].v3(8)[:, n8, :], ALU.mult, [pa, sgt[0][s]], [a1])
                op_tt(P, "dve", a2.h, pb.h[:, :], sgt[1][s].v3(8)[:, n8, :], ALU.mult, [pb, sgt[1][s]], [a2])
                op_tt(P, "pool", mst[s].v3(8)[:, n8, :], a1.h, a2.h, ALU.add, [a1, a2], [mst[s]])
            op_dma(P, "sp", MT_d[g * 8:(g + 1) * 8, :, ti * 512:(ti + 1) * 512].rearrange("k p t -> p k t"), mst[s].v3(8),
                   [mst[s]], [], mst[s])
    P.barrier()

    def own_row(r):
        return own_pos[r // 512] * 512 + (r % 512)

    def proj_res(srcT, w_ap, res_fn, dst_d):
        A.reset()
        wt2 = [A.bf("w2_%d" % i, KC * 1024) for i in range(2)]
        at2 = [A.bf("a2_%d" % i, KC * 512) for i in range(2)]
        rs = [A.f32("rs%d" % i, 1024) for i in range(4)]
        load_w(wt2[0], w_ap, 0, 1024, KC)
        load_w(wt2[1], w_ap, 1024, 1024, KC)
        c2 = 0
        c3 = 0
        apf2 = Prefetch(at2, 2 * NT, 1, lambda k, t_: load_at(t_, srcT, (k % NT) * 512, KC))
        rpf = Prefetch(rs, 2 * NT * 4, 2, lambda k, t_: op_dma(
            P, "sp", t_.h, res_fn(((k // 4) % NT) * 512 + (k % 4) * 128, (k // (4 * NT)) * 1024, 1024), [], [t_], t_))
        for g in range(2):
            wv = wt2[g].v3(KC)
            for ti in range(NT):
                a = apf2.get(c2)
                c2 += 1
                av = a.v3(KC)
                for bi in range(4):
                    r_t = rpf.get(c3)
                    c3 += 1
                    r0 = ti * 512 + bi * 128
                    for n2 in range(2):
                        bk = next_bank()
                        for kc in range(KC):
                            op_mm(P, bk.h[:, :], av[:, kc, bi * 128:(bi + 1) * 128], wv[:, kc, n2 * 512:(n2 + 1) * 512],
                                  kc == 0, kc == KC - 1, [wt2[g], a], [bk])
                        op_tt(P, "dve", r_t[:, n2 * 512:(n2 + 1) * 512], bk.h[:, :], r_t[:, n2 * 512:(n2 + 1) * 512], ALU.add,
                              [bk, r_t], [r_t])
                    op_dma(P, "sp", dst_d[r0:r0 + 128, g * 1024:(g + 1) * 1024], r_t.h, [r_t], [], r_t)
        P.barrier()

    proj_res(MT_d, w_out, lambda r0, c0, n: xw[own_row(r0):own_row(r0) + 128, c0:c0 + n], X1_d)
    def pre_ffn():
        pre_w["wu"] = [A.top_bf("wu%d" % i, KC * 1024) for i in range(2)]
        load_w(pre_w["wu"][0], w_up, 0, 1024, KC)

    normT(lambda b: X1_d[b * 128:(b + 1) * 128, :], NOWN // 128, g_mlp, hmT_d, pre=pre_ffn)

    A.reset()
    hm = [A.bf("hm%d" % i, KC * 512) for i in range(1)]
    AT = A.bf("AT", 64 * 512)
    wu = pre_w["wu"]
    wd = [A.bf("wd%d" % i, 8 * 512) for i in range(3)]
    rl = [A.f32("rl%d" % i, 512) for i in range(2)]
    xr = [A.f32("xr%d" % i, 512) for i in range(4)]
    cw = 0
    cd = 0
    cx = 0
    def issue_wd(k, t_):
        fg_ = k % 8
        nq_ = (k // 8) % 4
        op_dma(P, "pool", t_.v3(8),
               w_down[fg_ * 1024:(fg_ + 1) * 1024, nq_ * 512:(nq_ + 1) * 512].rearrange("(f p) n -> p f n", p=128),
               [], [t_], t_)

    wdpf = Prefetch(wd, NT * 32, 2, issue_wd)
    load_at(hm[0], hmT_d, 0, KC)
    for ti in range(NT):
        hm_t = hm[0]
        hv = hm_t.v3(KC)
        for g in range(8):
            w = wu[cw % 2]
            cw += 1
            ng = g + 1 if g < 7 else (0 if ti + 1 < NT else None)
            if ng is not None:
                load_w(wu[cw % 2], w_up, ng * 1024, 1024, KC)
            wv = w.v3(KC)
            for n8 in range(8):
                fc = g * 8 + n8
                bk = banks[fc % 2]
                for kc in range(KC):
                    op_mm(P, bk.h[:, :], wv[:, kc, n8 * 128:(n8 + 1) * 128], hv[:, kc, :], kc == 0, kc == KC - 1, [w, hm_t], [bk])
                r_t = rl[fc % 2]
                op_act(P, r_t.h, bk.h[:, :], AF.Relu, [bk], [r_t])
                op_tt(P, "pool", AT.h[:, fc * 512:(fc + 1) * 512], r_t.h, r_t.h, ALU.mult, [r_t], [AT])
        if ti + 1 < NT:
            load_at(hm[0], hmT_d, (ti + 1) * 512, KC)
        for nq in range(4):
            pbs = [banks[2 + bi] for bi in range(4)]
            for bi in range(4):
                r0 = ti * 512 + bi * 128
                op_dma(P, "sp", xr[bi].h, X1_d[r0:r0 + 128, nq * 512:(nq + 1) * 512], [], [xr[bi]], xr[bi])
            for fg in range(8):
                wd_t = wdpf.get(cd)
                cd += 1
                for f8 in range(8):
                    fc = fg * 8 + f8
                    for bi in range(4):
                        op_mm(P, pbs[bi].h[:, :], AT.h[:, fc * 512 + bi * 128: fc * 512 + (bi + 1) * 128], wd_t.v3(8)[:, f8, :],
                              fc == 0, fc == 63, [AT, wd_t], [pbs[bi]])
            for bi in range(4):
                x_t = xr[bi]
                r0 = ti * 512 + bi * 128
                op_tt(P, "dve", x_t.h, pbs[bi].h[:, :], x_t.h, ALU.add, [pbs[bi], x_t], [x_t])
                op_dma(P, "sp", X2_d[r0:r0 + 128, nq * 512:(nq + 1) * 512], x_t.h, [x_t], [], x_t)
    A.release_top()
    P.barrier()

    def pre_ple():
        pre_w["wg"] = [A.top_bf("wg%d" % i, KC * 1024) for i in range(2)]
        load_w(pre_w["wg"][0], w_pg, 0, 1024, KC)
        load_w(pre_w["wg"][1], w_pg, 1024, 1024, KC)
        pf = [A.f32("pf%d" % i, PLE) for i in range(2)]
        pb16 = [A.bf("pb%d" % i, PLE) for i in range(2)]
        pst = [A.bf("pst%d" % i, 2 * 512) for i in range(2)]
        for b in range(NOWN // 128):
            s = b % 2
            t, bi = b // 4, b % 4
            op_dma(P, "sp", pf[s].h, p_own[b * 128:(b + 1) * 128, :], [], [pf[s]], pf[s])
            op_copy(P, "dve", pb16[s].h, pf[s].h, [pf[s]], [pb16[s]])
            bk = banks[4 + b % 2]
            bb = bank_bf(bk)
            for k in range(2):
                op_tr(P, bb[:, k * 128:(k + 1) * 128], pb16[s][:, k * 128:(k + 1) * 128], ident.h, [pb16[s], ident], [bk])
            op_copy(P, "act", pst[t % 2].v3(2)[:, :, bi * 128:(bi + 1) * 128], bb[:, 0:256].rearrange("p (a b) -> p a b", a=2),
                    [bk], [pst[t % 2]])
            if bi == 3:
                op_dma(P, "sp", pT_d[:, :, t * 512:(t + 1) * 512].rearrange("k p t -> p k t"), pst[t % 2].v3(2),
                       [pst[t % 2]], [], pst[t % 2])

    normT(lambda b: X2_d[b * 128:(b + 1) * 128, :], NOWN // 128, g_ple, hpT_d, pre=pre_ple)
    A.reset()
    wg = pre_w["wg"]
    wpp = A.bf("wpp", 2 * D)
    ag = [A.bf("ag%d" % i, KC * 512) for i in range(2)]
    pt_ = [A.bf("pt%d" % i, 2 * 512) for i in range(2)]
    sgm = [A.f32("sgm%d" % i, 512) for i in range(2)]
    xo = [A.f32("xo%d" % i, 1024) for i in range(4)]
    op_dma(P, "pool", wpp.v3(2), w_pp.rearrange("(kc p) n -> p kc n", p=128), [], [wpp], wpp)
    c2 = 0
    c3 = 0
    c4 = 0
    def issue_g(k, s_):
        load_at(ag[s_], hpT_d, (k % NT) * 512, KC)
        load_at(pt_[s_], pT_d, (k % NT) * 512, 2)

    gpf = Prefetch([0, 1], 2 * NT, 1, issue_g)
    xpf2 = Prefetch(xo, 2 * NT * 4, 2, lambda k, t_: op_dma(
        P, "sp", t_.h, X2_d[((k // 4) % NT) * 512 + (k % 4) * 128:((k // 4) % NT) * 512 + (k % 4) * 128 + 128,
                            (k // (4 * NT)) * 1024:(k // (4 * NT)) * 1024 + 1024], [], [t_], t_))
    for g in range(2):
        wv = wg[g].v3(KC)
        for ti in range(NT):
            s_ = gpf.get(c2)
            a = ag[s_]
            pp = pt_[s_]
            c2 += 1
            av = a.v3(KC)
            pv = pp.v3(2)
            for bi in range(4):
                x_t = xpf2.get(c3)
                c3 += 1
                r0 = ti * 512 + bi * 128
                for n2 in range(2):
                    col = g * 1024 + n2 * 512
                    bg = banks[(2 * c4) % 4]
                    bp = banks[(2 * c4 + 1) % 4]
                    sg_t = sgm[c4 % 2]
                    c4 += 1
                    for kc in range(KC):
                        op_mm(P, bg.h[:, :], av[:, kc, bi * 128:(bi + 1) * 128], wv[:, kc, n2 * 512:(n2 + 1) * 512],
                              kc == 0, kc == KC - 1, [wg[g], a], [bg])
                    for kc in range(2):
                        op_mm(P, bp.h[:, :], pv[:, kc, bi * 128:(bi + 1) * 128], wpp.v3(2)[:, kc, col:col + 512],
                              kc == 0, kc == 1, [wpp, pp], [bp])
                    op_act(P, sg_t.h, bg.h[:, :], AF.Sigmoid, [bg], [sg_t])
                    op_tt(P, "dve", sg_t.h, bp.h[:, :], sg_t.h, ALU.mult, [bp, sg_t], [sg_t])
                    op_tt(P, "pool", x_t[:, n2 * 512:(n2 + 1) * 512], x_t[:, n2 * 512:(n2 + 1) * 512], sg_t.h, ALU.add,
                          [x_t, sg_t], [x_t])
                op_dma(P, "sp", out_own[r0:r0 + 128, g * 1024:(g + 1) * 1024], x_t.h, [x_t], [], x_t)
    P.barrier()
    P.emit()
    return nc, P


def _consts():
    tk = np.arange(128)[:, None]
    tq = np.arange(128)[None, :]
    ident = np.eye(128, dtype=np.float32)
    tri = (tk >= tq).astype(np.float32)
    slow = (tk < tq).astype(np.float32)
    mall = np.zeros((128, 23, 128), np.float32)
    for k in range(-3, 20):
        dl = 128 * k + tq - tk
        c = ((dl >= 0) & (dl <= 128)).astype(np.float32)
        c += ((dl >= 0) & (dl <= 512) & (dl % 4 == 0)).astype(np.float32)
        c += ((dl >= 0) & (dl <= 2048) & (dl % 16 == 0)).astype(np.float32)
        mall[:, k + 3, :] = c
    msb = np.zeros((128, 4, 4, 128), np.float32)
    for mrel in range(4):
        for qb in range(4):
            msb[:, mrel, qb, :] = ((128 * mrel + tk) < (128 * qb + tq)).astype(np.float32)
    return ident, tri, slow, mall.reshape(128, -1), msb.reshape(128, -1)


def _host_inputs(inputs, NSLOT, cores):
    x = np.asarray(inputs["x"], np.float32)
    p = np.asarray(inputs["p"], np.float32)[0]
    WSP = 4 * NSLOT
    SW = 512 * WSP
    ident, tri, slow, mall, msb = _consts()
    shared = {
        "g_mix": np.asarray(inputs["g_mix"], np.float32)[0][None, :],
        "g_mlp": np.asarray(inputs["g_mlp"], np.float32)[0][None, :],
        "g_ple": np.asarray(inputs["g_ple"], np.float32)[0][None, :],
        "qn_gain": np.asarray(inputs["qn_gain"], np.float32)[0][None, :],
        "kn_gain": np.asarray(inputs["kn_gain"], np.float32)[0][None, :],
        "w_in": np.asarray(inputs["w_in"], np.float32)[0],
        "w_branch_a": np.asarray(inputs["w_branch_a"], np.float32)[0],
        "w_branch_b": np.asarray(inputs["w_branch_b"], np.float32)[0],
        "w_out": np.asarray(inputs["w_out"], np.float32)[0],
        "w_up": np.asarray(inputs["w_up"], np.float32)[0],
        "w_down": np.asarray(inputs["w_down"], np.float32)[0],
        "w_ple_gate": np.asarray(inputs["w_ple_gate"], np.float32)[0],
        "w_ple_proj": np.asarray(inputs["w_ple_proj"], np.float32)[0],
        "c_ident": ident, "c_tri": tri, "c_slow": slow, "c_mall": mall, "c_msb": msb,
    }
    inv = (np.float32(500000.0) ** (-np.arange(0, 32, 2, dtype=np.float32) / np.float32(32))).astype(np.float32)
    maps = []
    for (b, j) in cores:
        pad = (3 - j) * 512
        nreal = SW - pad
        xwin = np.zeros((SW, D), np.float32)
        xwin[pad:] = x[b, :nreal]
        pos = (np.arange(SW) - pad).astype(np.float32)
        ang = pos[:, None] * inv[None, :]
        cs = np.concatenate([np.cos(ang), np.sin(ang)], axis=1).astype(np.float32)
        own_rows = np.concatenate([np.arange((4 * i + j) * 512, (4 * i + j + 1) * 512) for i in range(NSLOT)])
        valid = np.zeros((128, 12, 128), np.float32)
        for kb in range(12):
            if kb * 128 >= pad:
                valid[:, kb, :] = 1.0
        m = dict(shared)
        m["xw"] = xwin
        m["p_own"] = np.ascontiguousarray(p[b, own_rows])
        m["cs_tab"] = cs
        m["c_valid"] = valid.reshape(128, -1)
        maps.append(m)
    return maps


_CACHE = {}


def kernel(**inputs):
    NSLOT = 4
    if "nc" not in _CACHE:
        _CACHE["nc"] = build(NSLOT)[0]
    nc = _CACHE["nc"]
    cores = [(c // 4, c % 4) for c in range(8)]
    maps = _host_inputs(inputs, NSLOT, cores)
    res = run_bass_kernel_spmd(nc, maps, core_ids=list(range(8)))
    x = np.asarray(inputs["x"])
    out = np.empty(x.shape, np.float32)
    for c, (b, j) in enumerate(cores):
        o = res.results[c]["out_own"]
        for i in range(NSLOT):
            s = 4 * i + j
            out[b, s * 512:(s + 1) * 512] = o[i * 512:(i + 1) * 512]
    return out
```

```python
import contextlib
import bisect
import numpy as np
import concourse.bass as bass
import concourse.mybir as mybir
from concourse.bass_utils import run_bass_kernel_spmd

F32 = mybir.dt.float32
BF16 = mybir.dt.bfloat16
AF = mybir.ActivationFunctionType
ALU = mybir.AluOpType
AX = mybir.AxisListType

D = 2048
KC = 16
DFF = 8192
HD = 128
NH = 8
PLE = 256
EPS = 1e-6
SCALE = HD ** -0.5
ENG = ["pe", "act", "dve", "pool", "sp"]
KNOB = {"LA_A": 3, "LA_B": 2, "LA_Q": 2, "PRUNE": True}


class T:
    __slots__ = ("name", "h", "last_w", "readers")

    def __init__(self, name, h=None):
        self.name = name
        self.h = h
        self.last_w = None
        self.readers = []

    def __getitem__(self, k):
        return self.h[k]

    def v3(self, a):
        return self.h.rearrange("p (a b) -> p a b", a=a)


class Op:
    __slots__ = ("eng", "fn", "deps", "dma", "semkey", "idx", "needed", "cnt")


class Prog:
    def __init__(self, nc):
        self.nc = nc
        self.ops = []
        self.barriers = []
        self.stack = contextlib.ExitStack()

    def dram(self, name, shape, dt, kind="Internal"):
        return self.nc.dram_tensor(name, list(shape), dt, kind=kind).ap()

    def add(self, eng, fn, reads=(), writes=(), dma=False, semkey=None):
        op = Op()
        op.eng, op.fn, op.dma, op.semkey = eng, fn, dma, semkey
        op.deps = set()
        op.needed = False
        op.cnt = 0
        op.idx = len(self.ops)
        for t in reads:
            if t.last_w is not None:
                op.deps.add(t.last_w)
            if dma or not KNOB['PRUNE']:
                t.readers.append(op)
            else:
                t.readers = [r for r in t.readers if r.dma or r.eng != eng]
                t.readers.append(op)
        for t in writes:
            if t.last_w is not None:
                op.deps.add(t.last_w)
            for r in t.readers:
                if r is not op:
                    op.deps.add(r)
            t.last_w = op
            t.readers = []
        assert not (dma and semkey is None)
        self.ops.append(op)
        return op

    def barrier(self):
        self.barriers.append(len(self.ops))

    def emit(self):
        nc = self.nc
        ops = self.ops
        def phase_of(idx):
            return bisect.bisect_right(self.barriers, idx)
        for op in ops:
            if op.eng == "pe" and not op.dma:
                op.deps = {d for d in op.deps if not (d.eng == "pe" and not d.dma)}
            op.deps.discard(op)
            ph = phase_of(op.idx)
            op.deps = {d for d in op.deps if phase_of(d.idx) == ph}
            for d in op.deps:
                d.needed = True
        bar_last = []
        for pos in self.barriers:
            last = {}
            for op in ops[:pos][::-1]:
                if not op.dma and op.eng not in last:
                    last[op.eng] = op
                    if len(last) == 4:
                        break
            bar_last.append(last)
        for last in bar_last:
            for op in last.values():
                op.needed = True
        engsem = {en: self.stack.enter_context(nc.semaphore("s_" + en)) for en in ENG}
        phys = []
        physcount = []
        physcls = []
        keymap = {}
        cnt = {en: 0 for en in ENG}
        per_sem = {}
        bpos = list(self.barriers)
        nbp = 0
        for op in ops:
            while nbp < len(bpos) and bpos[nbp] <= op.idx:
                keymap = {}
                nbp += 1
            if op.dma:
                k = id(op.semkey)
                if k not in keymap:
                    cls = "sw" if op.eng == "pool" else "hw"
                    used = [v for v in keymap.values() if physcls[v] == cls]
                    free = [pi_ for pi_ in range(len(phys)) if physcls[pi_] == cls and pi_ not in used]
                    if free:
                        pi = free[0]
                    else:
                        pi = len(phys)
                        phys.append(self.stack.enter_context(nc.semaphore("d%s_%d" % (cls, pi))))
                        physcount.append(0)
                        physcls.append(cls)
                    keymap[k] = pi
                pi = keymap[k]
                physcount[pi] += 16
                op.cnt = physcount[pi]
                op.semkey = pi
                per_sem.setdefault(pi, []).append((op.idx, op.cnt))
            elif op.needed:
                cnt[op.eng] += 1
                op.cnt = cnt[op.eng]
        self.n_sems = len(phys) + 5
        self.max_counts = (dict(cnt), list(physcount))
        dmasem = {pi: phys[pi] for pi in range(len(phys))}
        dma_order = list(range(len(phys)))
        per_sem_idx = {k: [a for a, _ in v] for k, v in per_sem.items()}
        bar_waits = []
        for bi, pos in enumerate(self.barriers):
            wl = []
            for en, op in bar_last[bi].items():
                wl.append((("e", en), engsem[en], op.cnt))
            for k in dma_order:
                lst = per_sem_idx[k]
                j = bisect.bisect_left(lst, pos) - 1
                if j >= 0:
                    wl.append((("d", k), dmasem[k], per_sem[k][j][1]))
            bar_waits.append(wl)
        streams = {en: [] for en in ENG}
        for op in ops:
            streams[op.eng].append(op)
        nbar = len(self.barriers)

        def run_stream(en, e):
            waited = {}
            nb = 0

            def do_wait(key, sem, c):
                if waited.get(key, 0) < c:
                    e.wait_ge(sem, c)
                    waited[key] = c

            for op in streams[en]:
                while nb < nbar and self.barriers[nb] <= op.idx:
                    for key, sem, c in bar_waits[nb]:
                        do_wait(key, sem, c)
                    nb += 1
                need = {}
                for d in op.deps:
                    if d.dma:
                        k = d.semkey
                        lst = per_sem_idx[k]
                        j = bisect.bisect_left(lst, op.idx) - 1
                        c = per_sem[k][j][1]
                        key, sem = ("d", k), dmasem[k]
                    else:
                        c = d.cnt
                        key, sem = ("e", d.eng), engsem[d.eng]
                    if need.get(key, (None, 0))[1] < c:
                        need[key] = (sem, c)
                for key, (sem, c) in need.items():
                    do_wait(key, sem, c)
                ins = op.fn(e)
                if op.dma:
                    ins.then_inc(dmasem[op.semkey], 16)
                elif op.needed:
                    ins.then_inc(engsem[en], 1)
            while nb < nbar:
                for key, sem, c in bar_waits[nb]:
                    do_wait(key, sem, c)
                nb += 1

        with nc.Block() as block:
            @block.tensor
            def _(e):
                run_stream("pe", e)

            @block.scalar
            def _(e):
                run_stream("act", e)

            @block.vector
            def _(e):
                run_stream("dve", e)

            @block.gpsimd
            def _(e):
                run_stream("pool", e)

            @block.sync
            def _(e):
                run_stream("sp", e)
        self.stack.close()


class Arena:
    def __init__(self, P, nbytes):
        self.P = P
        self.n = nbytes // 2
        self.h = P.stack.enter_context(P.nc.sbuf_tensor("arena", [128, self.n], BF16))
        self.base = 0
        self.off = 0
        self.top = self.n

    def freeze(self):
        self.base = self.off

    def top_bf(self, name, n):
        n2 = (n + 15) // 16 * 16
        self.top -= n2
        assert self.top >= self.off, ("arena top overflow", name)
        return T(name, self.h[:, self.top:self.top + n])

    def release_top(self):
        self.top = self.n

    def reset(self):
        self.off = self.base

    def bf(self, name, n):
        n2 = (n + 15) // 16 * 16
        assert self.off + n2 <= self.top, ("arena overflow", name, self.off, n2, self.top)
        t = T(name, self.h[:, self.off:self.off + n])
        self.off += n2
        return t

    def f32(self, name, n):
        m = (2 * n + 15) // 16 * 16
        assert self.off + m <= self.top, ("arena overflow", name, self.off, m, self.top)
        t = T(name, self.h[:, self.off:self.off + 2 * n].bitcast(F32))
        self.off += m
        return t


class Prefetch:
    def __init__(self, slots, n, dist, issue):
        assert dist < len(slots)
        self.slots, self.n, self.dist, self.issue = slots, n, dist, issue
        self.next = 0

    def get(self, k):
        while self.next < self.n and self.next <= k + self.dist:
            self.issue(self.next, self.slots[self.next % len(self.slots)])
            self.next += 1
        return self.slots[k % len(self.slots)]

def op_dma(P, eng, out_ap, in_ap, reads, writes, semkey):
    P.add(eng, lambda e: e.dma_start(out=out_ap, in_=in_ap), reads=reads, writes=writes, dma=True, semkey=semkey)


def op_mm(P, out_ap, lhsT, rhs, start, stop, reads, writes, skip=False):
    if skip:
        P.add("pe", lambda e: e.matmul(out_ap, lhsT=lhsT, rhs=rhs, start=start, stop=stop, skip_group_check=True),
              reads=reads, writes=writes)
    else:
        P.add("pe", lambda e: e.matmul(out_ap, lhsT=lhsT, rhs=rhs, start=start, stop=stop), reads=reads, writes=writes)


def op_tr(P, out_ap, in_ap, ident_ap, reads, writes):
    P.add("pe", lambda e: e.transpose(out=out_ap, in_=in_ap, identity=ident_ap), reads=reads, writes=writes)


def op_act(P, out_ap, in_ap, func, reads, writes, scale=1.0, bias=0.0, accum=None):
    if accum is None:
        P.add("act", lambda e: e.activation(out=out_ap, in_=in_ap, func=func, scale=scale, bias=bias), reads=reads, writes=writes)
    else:
        P.add("act", lambda e: e.activation(out=out_ap, in_=in_ap, func=func, scale=scale, bias=bias, accum_out=accum),
              reads=reads, writes=writes)


def op_tt(P, eng, out_ap, a_ap, b_ap, op, reads, writes):
    P.add(eng, lambda e: e.tensor_tensor(out=out_ap, in0=a_ap, in1=b_ap, op=op), reads=reads, writes=writes)


def op_copy(P, eng, out_ap, in_ap, reads, writes):
    if eng == "act":
        P.add("act", lambda e: e.activation(out=out_ap, in_=in_ap, func=AF.Copy), reads=reads, writes=writes)
    else:
        P.add(eng, lambda e: e.tensor_copy(out=out_ap, in_=in_ap), reads=reads, writes=writes)


def op_stt(P, eng, out_ap, in0, scalar, in1, op0, op1, reads, writes):
    P.add(eng, lambda e: e.scalar_tensor_tensor(out=out_ap, in0=in0, scalar=scalar, in1=in1, op0=op0, op1=op1),
          reads=reads, writes=writes)


def op_memset(P, eng, ap, val, writes):
    P.add(eng, lambda e: e.memset(ap, val), writes=writes)


def build(NSLOT=4, debug=False):
    WSP = 4 * NSLOT
    SW = 512 * WSP
    NOWN = 512 * NSLOT
    NBW = SW // 128
    own_pos = [4 * i + 3 for i in range(NSLOT)]

    nc = bass.Bass("TRN2", target_bir_lowering=False)
    P = Prog(nc)
    IN, OUT = "ExternalInput", "ExternalOutput"
    SCR = OUT if debug else "Internal"
    xw = P.dram("xw", [SW, D], F32, IN)
    p_own = P.dram("p_own", [NOWN, PLE], F32, IN)
    g_mix = P.dram("g_mix", [1, D], F32, IN)
    g_mlp = P.dram("g_mlp", [1, D], F32, IN)
    g_ple = P.dram("g_ple", [1, D], F32, IN)
    qn_g = P.dram("qn_gain", [1, HD], F32, IN)
    kn_g = P.dram("kn_gain", [1, HD], F32, IN)
    w_in = P.dram("w_in", [D, 10240], F32, IN)
    w_a = P.dram("w_branch_a", [1024, D], F32, IN)
    w_b = P.dram("w_branch_b", [1024, D], F32, IN)
    w_out = P.dram("w_out", [D, D], F32, IN)
    w_up = P.dram("w_up", [D, DFF], F32, IN)
    w_down = P.dram("w_down", [DFF, D], F32, IN)
    w_pg = P.dram("w_ple_gate", [D, D], F32, IN)
    w_pp = P.dram("w_ple_proj", [PLE, D], F32, IN)
    cs_d = P.dram("cs_tab", [SW, 32], F32, IN)
    c_ident = P.dram("c_ident", [128, 128], F32, IN)
    c_tri = P.dram("c_tri", [128, 128], F32, IN)
    c_slow = P.dram("c_slow", [128, 128], F32, IN)
    c_valid = P.dram("c_valid", [128, 12 * 128], F32, IN)
    c_mall = P.dram("c_mall", [128, 23 * 128], F32, IN)
    c_msb = P.dram("c_msb", [128, 4 * 512], F32, IN)
    out_own = P.dram("out_own", [NOWN, D], F32, OUT)

    hT_d = P.dram("hT_d", [KC, 128, SW], BF16, SCR)
    KAT_d = P.dram("KAT_d", [NH, 128, SW], BF16, SCR)
    KBT_d = P.dram("KBT_d", [NH, 128, SW], BF16, SCR)
    VA_d = P.dram("VA_d", [NH, 128, NBW, 128], BF16, SCR)
    VB_d = P.dram("VB_d", [NH, 128, NBW, 128], BF16, SCR)
    QAT_d = P.dram("QAT_d", [NH, 128, NOWN], BF16, SCR)
    QBT_d = P.dram("QBT_d", [NH, 128, NOWN], BF16, SCR)
    SG_d = P.dram("SG_d", [2, KC, 128, NOWN], BF16, SCR)
    YT_d = P.dram("YT_d", [2, NH, 128, NOWN], BF16, SCR)
    MT_d = P.dram("MT_d", [KC, 128, NOWN], BF16, SCR)
    X1_d = P.dram("X1_d", [NOWN, D], F32, SCR)
    hmT_d = P.dram("hmT_d", [KC, 128, NOWN], BF16, SCR)
    X2_d = P.dram("X2_d", [NOWN, D], F32, SCR)
    hpT_d = P.dram("hpT_d", [KC, 128, NOWN], BF16, SCR)
    pT_d = P.dram("pT_d", [2, 128, NOWN], BF16, SCR)

    A = Arena(P, 200 * 1024)
    banks = [T("bank%d" % i, P.stack.enter_context(nc.psum_tensor("bank%d" % i, [128, 512], F32))) for i in range(8)]

    def bank_bf(b):
        return b.h[:, :].bitcast(BF16)

    ident = A.bf("ident", 128)
    tri = A.bf("tri", 128)
    slow = A.bf("slow", 128)
    ones = A.bf("ones", 128)
    valid = A.bf("valid", 12 * 128)
    mall = A.bf("mall", 23 * 128)
    msb = A.bf("msb", 4 * 512)
    A.freeze()
    for t, src in ((ident, c_ident), (tri, c_tri), (slow, c_slow), (valid, c_valid), (mall, c_mall), (msb, c_msb)):
        op_dma(P, "pool", t.h, src, [], [t], t)
    op_memset(P, "dve", ones.h, 1.0, [ones])

    def load_w(t, w_ap, c0, ncols, kcn):
        op_dma(P, "pool", t.h[:, 0:kcn * ncols].rearrange("p (a b) -> p a b", a=kcn),
               w_ap[:, c0:c0 + ncols].rearrange("(kc p) n -> p kc n", p=128), [], [t], t)

    def load_at(t, srcT, t0, kcn):
        op_dma(P, "sp", t.h[:, 0:kcn * 512].rearrange("p (a b) -> p a b", a=kcn),
               srcT[:, :, t0:t0 + 512].rearrange("k p t -> p k t"), [], [t], t)

    def normT(src_rows, nblk, gain_ap, dstT, pre=None):
        A.reset()
        if pre is not None:
            pre()
        gbc = A.f32("gbc", D)
        xs3 = [A.f32("xs%d" % i, D) for i in range(3)]
        junk = A.bf("junk", D)
        ssq = [A.f32("ssq%d" % i, 8) for i in range(2)]
        hn = [A.bf("hn%d" % i, D) for i in range(2)]
        stage = [A.bf("stg%d" % i, KC * 512) for i in range(2)]
        op_dma(P, "sp", gbc.h, gain_ap.broadcast_to([128, D]), [], [gbc], gbc)
        xpf = Prefetch(xs3, nblk, 2, lambda k, t_: op_dma(P, "sp", t_.h, src_rows(k), [], [t_], t_))

        def nA(b):
            s = b % 2
            x_t = xpf.get(b)
            op_memset(P, "pool", ssq[s].h, 0.0, [ssq[s]])
            op_act(P, junk.h, x_t.h, AF.Square, [x_t], [junk, ssq[s]], accum=ssq[s][:, 0:1])
            op_act(P, ssq[s][:, 1:2], ssq[s][:, 0:1], AF.Ln, [ssq[s]], [ssq[s]], scale=1.0 / D, bias=EPS)
            op_act(P, ssq[s][:, 2:3], ssq[s][:, 1:2], AF.Exp, [ssq[s]], [ssq[s]], scale=-0.5)
            op_stt(P, "dve", hn[s].h, x_t.h, ssq[s][:, 2:3], gbc.h, ALU.mult, ALU.mult, [x_t, ssq[s], gbc], [hn[s]])

        def nB(b):
            s = b % 2
            t, bi = b // 4, b % 4
            st = stage[t % 2]
            for half in range(2):
                bk = banks[(2 * b + half) % 4]
                bb = bank_bf(bk)
                for k8 in range(8):
                    kc = half * 8 + k8
                    op_tr(P, bb[:, k8 * 128:(k8 + 1) * 128], hn[s][:, kc * 128:(kc + 1) * 128], ident.h, [hn[s], ident], [bk])
                dst = st.v3(KC)[:, half * 8:(half + 1) * 8, bi * 128:(bi + 1) * 128]
                op_copy(P, "act" if half == 0 else "dve", dst, bb.rearrange("p (a b) -> p a b", a=8), [bk], [st])
            if bi == 3 or b == nblk - 1:
                op_dma(P, "pool", dstT[:, :, t * 512:(t + 1) * 512].rearrange("k p t -> p k t"), st.v3(KC), [st], [], st)

        nA(0)
        for b in range(nblk):
            if b + 1 < nblk:
                nA(b + 1)
            nB(b)
        P.barrier()

    pre_w = {}

    def pre_phase1():
        pre_w["wt"] = [A.top_bf("wt%d" % i, KC * 1024) for i in range(2)]
        load_w(pre_w["wt"][0], w_in, 1024, 1024, KC)
        load_w(pre_w["wt"][1], w_in, 2048, 1024, KC)

    normT(lambda b: xw[b * 128:(b + 1) * 128, :], NBW, g_mix, hT_d, pre=pre_phase1)

    A.reset()
    wt = pre_w["wt"]
    at = [A.bf("at%d" % i, KC * 512) for i in range(3)]
    stg = [A.bf("pstg%d" % i, 8 * 512) for i in range(2)]
    sqs = [A.f32("sq%d" % i, 1024) for i in range(2)]
    kns = [A.f32("kn%d" % i, 1024) for i in range(2)]
    knb = [A.bf("knb%d" % i, 1024) for i in range(3)]
    s8 = [A.f32("s8_%d" % i, 32) for i in range(2)]
    cst = [A.f32("cs%d" % i, 32) for i in range(2)]
    rt = [A.f32("rt%d" % i, 128) for i in range(8)]
    qgb = A.f32("qgb", 128)
    kgb = A.f32("kgb", 128)
    op_dma(P, "sp", qgb.h, qn_g.broadcast_to([128, HD]), [], [qgb], qgb)
    op_dma(P, "sp", kgb.h, kn_g.broadcast_to([128, HD]), [], [kgb], kgb)

    all_tiles = list(range(WSP))
    own_tiles = own_pos
    passes = [
        ("ka", 1024, all_tiles, "qk", (kgb, KAT_d, False)),
        ("va", 2048, all_tiles, "v", VA_d),
        ("kb", 4096, all_tiles, "fm", (KBT_d, False)),
        ("vb", 5120, all_tiles, "v", VB_d),
        ("qa", 0, own_tiles, "qk", (qgb, QAT_d, True)),
        ("qb", 3072, own_tiles, "fm", (QBT_d, True)),
        ("ga0", 6144, own_tiles, "sg", (0, 0)),
        ("ga1", 7168, own_tiles, "sg", (0, 1)),
        ("gb0", 8192, own_tiles, "sg", (1, 0)),
        ("gb1", 9216, own_tiles, "sg", (1, 1)),
    ]

    cnt = {"at": 0, "bank": 0, "stg": 0, "blk": 0}

    def next_bank():
        b = banks[cnt["bank"] % 4]
        cnt["bank"] += 1
        return b

    def qk_pass(w, wv, tiles, arg):
        gb_t, dstT, is_own = arg
        blocks = [(ti, tpos, bi) for ti, tpos in enumerate(tiles) for bi in range(4)]
        tinfo = {}

        def qkA(idx):
            ti, tpos, bi = blocks[idx]
            if bi == 0:
                a = atpf.get(item_index[(cur["pi"], ti)])
                st = stg[cnt["stg"] % 2]
                cnt["stg"] += 1
                tinfo[ti] = (a, st)
            a, st = tinfo[ti]
            av = a.v3(KC)
            bb = cnt["blk"]
            cnt["blk"] += 1
            s8t, cs, kb_t = s8[bb % 2], cst[bb % 2], knb[bb % 3]
            sq_t, kn_t = sqs[bb % 2], kns[bb % 2]
            rts = rt[4 * (bb % 2):4 * (bb % 2) + 4]
            r0 = tpos * 512 + bi * 128
            op_dma(P, "sp", cs.h, cs_d[r0:r0 + 128, :], [], [cs], cs)
            bks = []
            for n2 in range(2):
                bk = next_bank()
                bks.append(bk)
                for kc in range(KC):
                    op_mm(P, bk.h[:, :], av[:, kc, bi * 128:(bi + 1) * 128], wv[:, kc, n2 * 512:(n2 + 1) * 512],
                          kc == 0, kc == KC - 1, [w, a], [bk])
                op_act(P, sq_t[:, n2 * 512:(n2 + 1) * 512], bk.h[:, :], AF.Square, [bk], [sq_t])
            P.add("dve", (lambda o, i: (lambda e: e.tensor_reduce(out=o, in_=i, axis=AX.X, op=ALU.add)))(
                s8t[:, 0:8], sq_t.v3(8)), reads=[sq_t], writes=[s8t])
            op_act(P, s8t[:, 8:16], s8t[:, 0:8], AF.Ln, [s8t], [s8t], scale=1.0 / HD, bias=EPS)
            op_act(P, s8t[:, 16:24], s8t[:, 8:16], AF.Exp, [s8t], [s8t], scale=-0.5)
            for n2 in range(2):
                op_tt(P, "dve", kn_t.h[:, n2 * 512:(n2 + 1) * 512].rearrange("p (a b) -> p a b", a=4),
                      bks[n2].h[:, :].rearrange("p (a b) -> p a b", a=4),
                      s8t[:, 16 + n2 * 4:16 + (n2 + 1) * 4].unsqueeze(2).broadcast_to([128, 4, 128]),
                      ALU.mult, [bks[n2], s8t], [kn_t])
            k3 = kn_t.v3(8)
            op_tt(P, "pool", k3, k3, gb_t.h.unsqueeze(1).broadcast_to([128, 8, 128]), ALU.mult, [kn_t, gb_t], [kn_t])
            kb3 = kb_t.v3(8)
            op_copy(P, "dve", kb3[:, :, 32:128], k3[:, :, 32:128], [kn_t], [kb_t])
            cosb = cs[:, 0:16].unsqueeze(1).broadcast_to([128, 8, 16])
            sinb = cs[:, 16:32].unsqueeze(1).broadcast_to([128, 8, 16])
            x1, x2 = k3[:, :, 0:16], k3[:, :, 16:32]
            r = [t_.v3(8) for t_ in rts]
            op_tt(P, "pool", r[0], x1, cosb, ALU.mult, [kn_t, cs], [rts[0]])
            op_tt(P, "pool", r[1], x2, sinb, ALU.mult, [kn_t, cs], [rts[1]])
            op_tt(P, "pool", r[2], x2, cosb, ALU.mult, [kn_t, cs], [rts[2]])
            op_tt(P, "pool", r[3], x1, sinb, ALU.mult, [kn_t, cs], [rts[3]])
            op_tt(P, "pool", kb3[:, :, 0:16], r[0], r[1], ALU.subtract, [rts[0], rts[1]], [kb_t])
            op_tt(P, "pool", kb3[:, :, 16:32], r[2], r[3], ALU.add, [rts[2], rts[3]], [kb_t])
            tinfo[("kb", idx)] = (kb_t, bb)

        def qkB(idx):
            ti, tpos, bi = blocks[idx]
            a, st = tinfo[ti]
            kb_t, bb = tinfo.pop(("kb", idx))
            kb3 = kb_t.v3(8)
            tb = banks[4 + bb % 2]
            tbb = bank_bf(tb)
            for h in range(NH):
                op_tr(P, tbb[:, h * 128:(h + 1) * 128], kb3[:, h, :], ident.h, [kb_t, ident], [tb])
            op_copy(P, "act" if bb % 2 == 0 else "dve", st.v3(8)[:, :, bi * 128:(bi + 1) * 128],
                    tbb.rearrange("p (a b) -> p a b", a=8), [tb], [st])
            if bi == 3:
                t0 = ti * 512 if is_own else tpos * 512
                op_dma(P, "sp", dstT[:, :, t0:t0 + 512].rearrange("h d t -> d h t"), st.v3(8), [st], [], st)

        nb_ = len(blocks)
        LQ = KNOB['LA_Q']
        for idx in range(min(LQ, nb_)):
            qkA(idx)
        for idx in range(nb_):
            if idx + LQ < nb_:
                qkA(idx + LQ)
            qkB(idx)

    items = [(pi_, ti_, tp_) for pi_, ps_ in enumerate(passes) for ti_, tp_ in enumerate(ps_[2])]
    item_index = {(pi_, ti_): k for k, (pi_, ti_, tp_) in enumerate(items)}
    atpf = Prefetch(at, len(items), 2, lambda k, t_: load_at(t_, hT_d, items[k][2] * 512, KC))
    cur = {"pi": 0}
    assert passes[0][1] == 1024 and passes[1][1] == 2048
    for pi, (pname, c0, tiles, mode, arg) in enumerate(passes):
        cur["pi"] = pi
        w = wt[pi % 2]
        if pi >= 1 and pi + 1 < len(passes):
            load_w(wt[(pi + 1) % 2], w_in, passes[pi + 1][1], 1024, KC)
        wv = w.v3(KC)
        if mode == "qk":
            qk_pass(w, wv, tiles, arg)
            continue
        for ti, tpos in enumerate(tiles):
            a = atpf.get(item_index[(pi, ti)])
            av = a.v3(KC)
            own_t0 = ti * 512
            st = stg[cnt["stg"] % 2]
            cnt["stg"] += 1
            if mode in ("fm", "sg"):
                for n8 in range(8):
                    bk = next_bank()
                    for kc in range(KC):
                        op_mm(P, bk.h[:, :], wv[:, kc, n8 * 128:(n8 + 1) * 128], av[:, kc, :], kc == 0, kc == KC - 1, [w, a], [bk])
                    dst = st.v3(8)[:, n8, :]
                    if mode == "fm":
                        op_copy(P, "act" if n8 % 2 == 0 else "dve", dst, bk.h[:, :], [bk], [st])
                    else:
                        op_act(P, dst, bk.h[:, :], AF.Sigmoid, [bk], [st])
                if mode == "fm":
                    dstT, is_own = arg
                    t0 = own_t0 if is_own else tpos * 512
                    op_dma(P, "sp", dstT[:, :, t0:t0 + 512].rearrange("h d t -> d h t"), st.v3(8), [st], [], st)
                else:
                    ab, half = arg
                    op_dma(P, "sp", SG_d[ab, half * 8:(half + 1) * 8, :, own_t0:own_t0 + 512].rearrange("k p t -> p k t"),
                           st.v3(8), [st], [], st)
            elif mode == "v":
                for bi in range(4):
                    for n2 in range(2):
                        bk = next_bank()
                        for kc in range(KC):
                            op_mm(P, bk.h[:, :], av[:, kc, bi * 128:(bi + 1) * 128], wv[:, kc, n2 * 512:(n2 + 1) * 512],
                                  kc == 0, kc == KC - 1, [w, a], [bk])
                        dst = st.h.rearrange("p (h b d) -> p h b d", h=NH, b=4)[:, n2 * 4:(n2 + 1) * 4, bi, :]
                        op_copy(P, "act" if n2 == 0 else "dve", dst, bk.h[:, :].rearrange("p (h d) -> p h d", h=4), [bk], [st])
                blk0 = tpos * 4
                op_dma(P, "sp", arg[:, :, blk0:blk0 + 4, :].rearrange("h p b d -> p h (b d)"),
                       st.h.rearrange("p (h x) -> p h x", h=NH), [st], [], st)
    A.release_top()
    P.barrier()

    A.reset()
    qt = [A.bf("qt%d" % i, 512) for i in range(2)]
    ktA = [A.bf("ktA%d" % i, 2560) for i in range(2)]
    vtA = [A.bf("vtA%d" % i, 20 * 128) for i in range(2)]
    ef = [A.bf("ef%d" % i, 512) for i in range(4)]
    ab = [A.bf("ab%d" % i, 512) for i in range(4)]
    rden = [A.f32("rden%d" % i, 512) for i in range(2)]
    yst = [A.bf("yst%d" % i, 512) for i in range(2)]
    it = 0
    gstep = [0]
    LA_A = KNOB['LA_A']

    def issue_a(k, s_):
        i_, h_ = k // NH, k % NH
        qb0_ = 4 * own_pos[i_]
        lo_ = max(0, qb0_ - 16)
        hi_ = qb0_ + 3
        n_ = hi_ - lo_ + 1
        op_dma(P, "sp", qt[s_].h, QAT_d[h_, :, i_ * 512:(i_ + 1) * 512], [], [qt[s_]], qt[s_])
        op_dma(P, "sp", ktA[s_].h[:, 0:n_ * 128], KAT_d[h_, :, lo_ * 128:(hi_ + 1) * 128], [], [ktA[s_]], ktA[s_])
        op_dma(P, "sp", vtA[s_].h[:, 0:n_ * 128].rearrange("p (b d) -> p b d", d=128), VA_d[h_, :, lo_:hi_ + 1, :],
               [], [vtA[s_]], vtA[s_])

    apf = Prefetch([0, 1], NSLOT * NH, 1, issue_a)
    for i in range(NSLOT):
        qb0 = 4 * own_pos[i]
        kb_lo = max(0, qb0 - 16)
        kb_hi = qb0 + 3
        nkb = kb_hi - kb_lo + 1
        for h in range(NH):
            s = it % 2
            apf.get(it)
            it += 1
            py = banks[4 + s]
            pd = banks[6 + s]
            g0 = gstep[0]

            def stA(m):
                step = g0 + m
                kabs = kb_lo + m
                ps = banks[step % 4]
                e_t = ef[step % 4]
                a_t = ab[step % 4]
                op_mm(P, ps.h[:, :], ktA[s][:, m * 128:(m + 1) * 128], qt[s].h, True, True, [ktA[s], qt[s]], [ps])
                op_act(P, e_t.h, ps.h[:, :], AF.Exp, [ps], [e_t], scale=SCALE)
                k0 = qb0 - kabs
                msl = mall.h[:, (k0 + 3) * 128:(k0 + 7) * 128]
                op_tt(P, "dve" if step % 4 != 3 else "pool", a_t.h, e_t.h, msl, ALU.mult, [e_t, mall], [a_t])

            def stB(m):
                step = g0 + m
                kabs = kb_lo + m
                a_t = ab[step % 4]
                op_mm(P, py.h[:, :], vtA[s][:, m * 128:(m + 1) * 128], a_t.h, m == 0, m == nkb - 1, [vtA[s], a_t], [py])
                if kabs < 12:
                    dl, dl_t = valid.h[:, kabs * 128:(kabs + 1) * 128], valid
                else:
                    dl, dl_t = ones.h, ones
                op_mm(P, pd.h[:, :], dl, a_t.h, m == 0, m == nkb - 1, [dl_t, a_t], [pd])

            for m in range(min(LA_A, nkb)):
                stA(m)
            for m in range(nkb):
                if m + LA_A < nkb:
                    stA(m + LA_A)
                stB(m)
            gstep[0] += nkb
            rd = rden[s]
            P.add("dve", (lambda o, i_: (lambda e: e.reciprocal(out=o, in_=i_)))(rd.h, pd.h[:, :]), reads=[pd], writes=[rd])
            op_tt(P, "dve", yst[s].h, py.h[:, :], rd.h, ALU.mult, [py, rd], [yst[s]])
            op_dma(P, "sp", YT_d[0, h, :, i * 512:(i + 1) * 512], yst[s].h, [yst[s]], [], yst[s])
    P.barrier()

    A.reset()
    NKMAX = 4 * (own_pos[-1] + 1)
    qtb = [[A.bf("qtb%d%d" % (u, i), 512) for i in range(2)] for u in range(2)]
    ktB = [[A.bf("ktB%d%d" % (u, i), NKMAX * 128) for i in range(2)] for u in range(2)]
    vtB = [[A.bf("vtB%d%d" % (u, i), NKMAX * 128) for i in range(2)] for u in range(2)]
    LA_B = KNOB['LA_B']
    NB_E = LA_B + 3
    e1 = [[A.f32("e1_%d%d" % (u, i), 512) for i in range(NB_E)] for u in range(2)]
    lp = [[A.bf("lp%d%d" % (u, i), 512) for i in range(NB_E)] for u in range(2)]
    wf = [[A.f32("wf%d%d" % (u, i), 512) for i in range(2)] for u in range(2)]
    aB = [[A.bf("aB%d%d" % (u, i), 512) for i in range(3)] for u in range(2)]
    ysb = [[A.bf("ysb%d%d" % (u, i), 512) for i in range(2)] for u in range(2)]
    itb = 0

    def issue_b(k, sl_):
        i_, hp_ = k // (NH // 2), k % (NH // 2)
        nk_ = 4 * (own_pos[i_] + 1)
        for u in range(2):
            h_ = 2 * hp_ + u
            op_dma(P, "sp", qtb[u][sl_].h, QBT_d[h_, :, i_ * 512:(i_ + 1) * 512], [], [qtb[u][sl_]], qtb[u][sl_])
            op_dma(P, "sp", ktB[u][sl_].h[:, 0:nk_ * 128], KBT_d[h_, :, 0:nk_ * 128], [], [ktB[u][sl_]], ktB[u][sl_])
            op_dma(P, "sp", vtB[u][sl_].h[:, 0:nk_ * 128].rearrange("p (b d) -> p b d", d=128), VB_d[h_, :, 0:nk_, :],
                   [], [vtB[u][sl_]], vtB[u][sl_])

    bpf = Prefetch([0, 1], NSLOT * (NH // 2), 1, issue_b)
    for i in range(NSLOT):
        nk = 4 * (own_pos[i] + 1)
        for hp in range(NH // 2):
            sl = itb % 2
            bpf.get(itb)
            itb += 1
            def sbA(st_i):
                m = nk - 1 - st_i
                mrel = m - (nk - 4)
                for u in range(2):
                    q_t, k_t = qtb[u][sl], ktB[u][sl]
                    pz = banks[4 * u + st_i % 2]
                    op_mm(P, pz.h[:, :], k_t[:, m * 128:(m + 1) * 128], q_t.h, True, True, [k_t, q_t], [pz])
                for u in range(2):
                    pz = banks[4 * u + st_i % 2]
                    e_t = e1[u][st_i % NB_E]
                    op_act(P, e_t.h, pz.h[:, :], AF.Exp, [pz], [e_t], scale=SCALE)
                    if mrel >= 0:
                        op_tt(P, "dve", e_t.h, e_t.h, msb.h[:, mrel * 512:(mrel + 1) * 512], ALU.mult, [e_t, msb], [e_t])
                for u in range(2):
                    e_t = e1[u][st_i % NB_E]
                    l_t = lp[u][st_i % NB_E]
                    op_act(P, l_t.h, e_t.h, AF.Ln, [e_t], [l_t], bias=1.0)

            def sbB(st_i):
                for u in range(2):
                    pR = banks[4 * u + 2]
                    l_t = lp[u][st_i % NB_E]
                    l_prev = lp[u][(st_i - 1) % NB_E]
                    if st_i > 0:
                        op_mm(P, pR.h[:, :], slow.h, l_prev.h, False, False, [slow, l_prev], [pR], skip=True)
                    op_mm(P, pR.h[:, :], tri.h, l_t.h, st_i == 0, (st_i == nk - 1) or KNOB.get('RSTOP', True), [tri, l_t], [pR], skip=True)
                for u in range(2):
                    pR = banks[4 * u + 2]
                    e_t = e1[u][st_i % NB_E]
                    w_t = wf[u][st_i % 2]
                    a_t = aB[u][st_i % 3]
                    op_act(P, w_t.h, pR.h[:, :], AF.Exp, [pR], [w_t], scale=-1.0)
                    op_tt(P, "pool" if u == 0 else "dve", a_t.h, e_t.h, w_t.h, ALU.mult, [e_t, w_t], [a_t])

            def sbC(st_i):
                m = nk - 1 - st_i
                for u in range(2):
                    v_t = vtB[u][sl]
                    pY = banks[4 * u + 3]
                    a_t = aB[u][st_i % 3]
                    op_mm(P, pY.h[:, :], v_t[:, m * 128:(m + 1) * 128], a_t.h, st_i == 0, st_i == nk - 1, [v_t, a_t], [pY])

            for st_i in range(min(LA_B, nk)):
                sbA(st_i)
            for st_i in range(nk):
                if st_i + LA_B < nk:
                    sbA(st_i + LA_B)
                sbB(st_i)
                if st_i >= 1:
                    sbC(st_i - 1)
            sbC(nk - 1)
            for u in range(2):
                h = 2 * hp + u
                op_copy(P, "dve", ysb[u][sl].h, banks[4 * u + 3].h[:, :], [banks[4 * u + 3]], [ysb[u][sl]])
                op_dma(P, "sp", YT_d[1, h, :, i * 512:(i + 1) * 512], ysb[u][sl].h, [ysb[u][sl]], [], ysb[u][sl])
    P.barrier()

    A.reset()
    NT = NOWN // 512
    wab = [[A.bf("wab%d%d" % (x, i), 8 * 1024) for i in range(2)] for x in range(2)]
    yt = [[A.bf("yt%d%d" % (x, i), 8 * 512) for i in range(2)] for x in range(2)]
    sgt = [[A.bf("sgt%d%d" % (x, i), 8 * 512) for i in range(2)] for x in range(2)]
    t1 = [A.f32("t1_%d" % i, 512) for i in range(2)]
    t2 = [A.f32("t2_%d" % i, 512) for i in range(2)]
    mst = [A.bf("mst%d" % i, 8 * 512) for i in range(2)]
    it3 = 0

    def issue_c(k, s_):
        g_, ti_ = k // NT, k % NT
        for x in range(2):
            op_dma(P, "sp", yt[x][s_].v3(8), YT_d[x, :, :, ti_ * 512:(ti_ + 1) * 512].rearrange("k p t -> p k t"),
                   [], [yt[x][s_]], yt[x][s_])
            op_dma(P, "sp", sgt[x][s_].v3(8),
                   SG_d[x, g_ * 8:(g_ + 1) * 8, :, ti_ * 512:(ti_ + 1) * 512].rearrange("k p t -> p k t"),
                   [], [sgt[x][s_]], sgt[x][s_])

    cpf = Prefetch([0, 1], 2 * NT, 1, issue_c)
    for x, wsrc in ((0, w_a), (1, w_b)):
        load_w(wab[x][0], wsrc, 0, 1024, 8)
        load_w(wab[x][1], wsrc, 1024, 1024, 8)
    for g in range(2):
        for x, wsrc in ():
            load_w(wab[x][g % 2], wsrc, g * 1024, 1024, 8)
        for ti in range(NT):
            s = it3 % 2
            cpf.get(it3)
            it3 += 1
            for n8 in range(8):
                pa = banks[(2 * n8) % 4]
                pb = banks[(2 * n8 + 1) % 4]
                for x, pbk in ((0, pa), (1, pb)):
                    wv = wab[x][g % 2].v3(8)
                    yv = yt[x][s].v3(8)
                    for kc in range(8):
                        op_mm(P, pbk.h[:, :], wv[:, kc, n8 * 128:(n8 + 1) * 128], yv[:, kc, :], kc == 0, kc == 7,
                              [wab[x][g % 2], yt[x][s]], [pbk])
                a1, a2 = t1[n8 % 2], t2[n8 % 2]
                op_tt(P, "dve", a1.h, pa.h[:, :], sgt[0][s].v3(8)[:, n8, :], ALU.mult, [pa, sgt[0][s]], [a1])
                op_tt(P, "dve", a2.h, pb.h[:, :], sgt[1][s].v3(8)[:, n8, :], ALU.mult, [pb, sgt[1][s]], [a2])
                op_tt(P, "pool", mst[s].v3(8)[:, n8, :], a1.h, a2.h, ALU.add, [a1, a2], [mst[s]])
            op_dma(P, "sp", MT_d[g * 8:(g + 1) * 8, :, ti * 512:(ti + 1) * 512].rearrange("k p t -> p k t"), mst[s].v3(8),
                   [mst[s]], [], mst[s])
    P.barrier()

    def own_row(r):
        return own_pos[r // 512] * 512 + (r % 512)

    def proj_res(srcT, w_ap, res_fn, dst_d):
        A.reset()
        wt2 = [A.bf("w2_%d" % i, KC * 1024) for i in range(2)]
        at2 = [A.bf("a2_%d" % i, KC * 512) for i in range(2)]
        rs = [A.f32("rs%d" % i, 1024) for i in range(4)]
        load_w(wt2[0], w_ap, 0, 1024, KC)
        load_w(wt2[1], w_ap, 1024, 1024, KC)
        c2 = 0
        c3 = 0
        apf2 = Prefetch(at2, 2 * NT, 1, lambda k, t_: load_at(t_, srcT, (k % NT) * 512, KC))
        rpf = Prefetch(rs, 2 * NT * 4, 2, lambda k, t_: op_dma(
            P, "sp", t_.h, res_fn(((k // 4) % NT) * 512 + (k % 4) * 128, (k // (4 * NT)) * 1024, 1024), [], [t_], t_))
        for g in range(2):
            wv = wt2[g].v3(KC)
            for ti in range(NT):
                a = apf2.get(c2)
                c2 += 1
                av = a.v3(KC)
                for bi in range(4):
                    r_t = rpf.get(c3)
                    c3 += 1
                    r0 = ti * 512 + bi * 128
                    for n2 in range(2):
                        bk = next_bank()
                        for kc in range(KC):
                            op_mm(P, bk.h[:, :], av[:, kc, bi * 128:(bi + 1) * 128], wv[:, kc, n2 * 512:(n2 + 1) * 512],
                                  kc == 0, kc == KC - 1, [wt2[g], a], [bk])
                        op_tt(P, "dve", r_t[:, n2 * 512:(n2 + 1) * 512], bk.h[:, :], r_t[:, n2 * 512:(n2 + 1) * 512], ALU.add,
                              [bk, r_t], [r_t])
                    op_dma(P, "sp", dst_d[r0:r0 + 128, g * 1024:(g + 1) * 1024], r_t.h, [r_t], [], r_t)
        P.barrier()

    proj_res(MT_d, w_out, lambda r0, c0, n: xw[own_row(r0):own_row(r0) + 128, c0:c0 + n], X1_d)
    def pre_ffn():
        pre_w["wu"] = [A.top_bf("wu%d" % i, KC * 1024) for i in range(2)]
        load_w(pre_w["wu"][0], w_up, 0, 1024, KC)

    normT(lambda b: X1_d[b * 128:(b + 1) * 128, :], NOWN // 128, g_mlp, hmT_d, pre=pre_ffn)

    A.reset()
    hm = [A.bf("hm%d" % i, KC * 512) for i in range(1)]
    AT = A.bf("AT", 64 * 512)
    wu = pre_w["wu"]
    wd = [A.bf("wd%d" % i, 8 * 512) for i in range(3)]
    rl = [A.f32("rl%d" % i, 512) for i in range(2)]
    xr = [A.f32("xr%d" % i, 512) for i in range(4)]
    cw = 0
    cd = 0
    cx = 0
    def issue_wd(k, t_):
        fg_ = k % 8
        nq_ = (k // 8) % 4
        op_dma(P, "pool", t_.v3(8),
               w_down[fg_ * 1024:(fg_ + 1) * 1024, nq_ * 512:(nq_ + 1) * 512].rearrange("(f p) n -> p f n", p=128),
               [], [t_], t_)

    wdpf = Prefetch(wd, NT * 32, 2, issue_wd)
    load_at(hm[0], hmT_d, 0, KC)
    for ti in range(NT):
        hm_t = hm[0]
        hv = hm_t.v3(KC)
        for g in range(8):
            w = wu[cw % 2]
            cw += 1
            ng = g + 1 if g < 7 else (0 if ti + 1 < NT else None)
            if ng is not None:
                load_w(wu[cw % 2], w_up, ng * 1024, 1024, KC)
            wv = w.v3(KC)
            for n8 in range(8):
                fc = g * 8 + n8
                bk = banks[fc % 2]
                for kc in range(KC):
                    op_mm(P, bk.h[:, :], wv[:, kc, n8 * 128:(n8 + 1) * 128], hv[:, kc, :], kc == 0, kc == KC - 1, [w, hm_t], [bk])
                r_t = rl[fc % 2]
                op_act(P, r_t.h, bk.h[:, :], AF.Relu, [bk], [r_t])
                op_tt(P, "pool", AT.h[:, fc * 512:(fc + 1) * 512], r_t.h, r_t.h, ALU.mult, [r_t], [AT])
        if ti + 1 < NT:
            load_at(hm[0], hmT_d, (ti + 1) * 512, KC)
        for nq in range(4):
            pbs = [banks[2 + bi] for bi in range(4)]
            for bi in range(4):
                r0 = ti * 512 + bi * 128
                op_dma(P, "sp", xr[bi].h, X1_d[r0:r0 + 128, nq * 512:(nq + 1) * 512], [], [xr[bi]], xr[bi])
            for fg in range(8):
                wd_t = wdpf.get(cd)
                cd += 1
                for f8 in range(8):
                    fc = fg * 8 + f8
                    for bi in range(4):
                        op_mm(P, pbs[bi].h[:, :], AT.h[:, fc * 512 + bi * 128: fc * 512 + (bi + 1) * 128], wd_t.v3(8)[:, f8, :],
                              fc == 0, fc == 63, [AT, wd_t], [pbs[bi]])
            for bi in range(4):
                x_t = xr[bi]
                r0 = ti * 512 + bi * 128
                op_tt(P, "dve", x_t.h, pbs[bi].h[:, :], x_t.h, ALU.add, [pbs[bi], x_t], [x_t])
                op_dma(P, "sp", X2_d[r0:r0 + 128, nq * 512:(nq + 1) * 512], x_t.h, [x_t], [], x_t)
    A.release_top()
    P.barrier()

    def pre_ple():
        pre_w["wg"] = [A.top_bf("wg%d" % i, KC * 1024) for i in range(2)]
        load_w(pre_w["wg"][0], w_pg, 0, 1024, KC)
        load_w(pre_w["wg"][1], w_pg, 1024, 1024, KC)
        pf = [A.f32("pf%d" % i, PLE) for i in range(2)]
        pb16 = [A.bf("pb%d" % i, PLE) for i in range(2)]
        pst = [A.bf("pst%d" % i, 2 * 512) for i in range(2)]
        for b in range(NOWN // 128):
            s = b % 2
            t, bi = b // 4, b % 4
            op_dma(P, "sp", pf[s].h, p_own[b * 128:(b + 1) * 128, :], [], [pf[s]], pf[s])
            op_copy(P, "dve", pb16[s].h, pf[s].h, [pf[s]], [pb16[s]])
            bk = banks[4 + b % 2]
            bb = bank_bf(bk)
            for k in range(2):
                op_tr(P, bb[:, k * 128:(k + 1) * 128], pb16[s][:, k * 128:(k + 1) * 128], ident.h, [pb16[s], ident], [bk])
            op_copy(P, "act", pst[t % 2].v3(2)[:, :, bi * 128:(bi + 1) * 128], bb[:, 0:256].rearrange("p (a b) -> p a b", a=2),
                    [bk], [pst[t % 2]])
            if bi == 3:
                op_dma(P, "sp", pT_d[:, :, t * 512:(t + 1) * 512].rearrange("k p t -> p k t"), pst[t % 2].v3(2),
                       [pst[t % 2]], [], pst[t % 2])

    normT(lambda b: X2_d[b * 128:(b + 1) * 128, :], NOWN // 128, g_ple, hpT_d, pre=pre_ple)
    A.reset()
    wg = pre_w["wg"]
    wpp = A.bf("wpp", 2 * D)
    ag = [A.bf("ag%d" % i, KC * 512) for i in range(2)]
    pt_ = [A.bf("pt%d" % i, 2 * 512) for i in range(2)]
    sgm = [A.f32("sgm%d" % i, 512) for i in range(2)]
    xo = [A.f32("xo%d" % i, 1024) for i in range(4)]
    op_dma(P, "pool", wpp.v3(2), w_pp.rearrange("(kc p) n -> p kc n", p=128), [], [wpp], wpp)
    c2 = 0
    c3 = 0
    c4 = 0
    def issue_g(k, s_):
        load_at(ag[s_], hpT_d, (k % NT) * 512, KC)
        load_at(pt_[s_], pT_d, (k % NT) * 512, 2)

    gpf = Prefetch([0, 1], 2 * NT, 1, issue_g)
    xpf2 = Prefetch(xo, 2 * NT * 4, 2, lambda k, t_: op_dma(
        P, "sp", t_.h, X2_d[((k // 4) % NT) * 512 + (k % 4) * 128:((k // 4) % NT) * 512 + (k % 4) * 128 + 128,
                            (k // (4 * NT)) * 1024:(k // (4 * NT)) * 1024 + 1024], [], [t_], t_))
    for g in range(2):
        wv = wg[g].v3(KC)
        for ti in range(NT):
            s_ = gpf.get(c2)
            a = ag[s_]
            pp = pt_[s_]
            c2 += 1
            av = a.v3(KC)
            pv = pp.v3(2)
            for bi in range(4):
                x_t = xpf2.get(c3)
                c3 += 1
                r0 = ti * 512 + bi * 128
                for n2 in range(2):
                    col = g * 1024 + n2 * 512
                    bg = banks[(2 * c4) % 4]
                    bp = banks[(2 * c4 + 1) % 4]
                    sg_t = sgm[c4 % 2]
                    c4 += 1
                    for kc in range(KC):
                        op_mm(P, bg.h[:, :], av[:, kc, bi * 128:(bi + 1) * 128], wv[:, kc, n2 * 512:(n2 + 1) * 512],
                              kc == 0, kc == KC - 1, [wg[g], a], [bg])
                    for kc in range(2):
                        op_mm(P, bp.h[:, :], pv[:, kc, bi * 128:(bi + 1) * 128], wpp.v3(2)[:, kc, col:col + 512],
                              kc == 0, kc == 1, [wpp, pp], [bp])
                    op_act(P, sg_t.h, bg.h[:, :], AF.Sigmoid, [bg], [sg_t])
                    op_tt(P, "dve", sg_t.h, bp.h[:, :], sg_t.h, ALU.mult, [bp, sg_t], [sg_t])
                    op_tt(P, "pool", x_t[:, n2 * 512:(n2 + 1) * 512], x_t[:, n2 * 512:(n2 + 1) * 512], sg_t.h, ALU.add,
                          [x_t, sg_t], [x_t])
                op_dma(P, "sp", out_own[r0:r0 + 128, g * 1024:(g + 1) * 1024], x_t.h, [x_t], [], x_t)
    P.barrier()
    P.emit()
    return nc, P


def _consts():
    tk = np.arange(128)[:, None]
    tq = np.arange(128)[None, :]
    ident = np.eye(128, dtype=np.float32)
    tri = (tk >= tq).astype(np.float32)
    slow = (tk < tq).astype(np.float32)
    mall = np.zeros((128, 23, 128), np.float32)
    for k in range(-3, 20):
        dl = 128 * k + tq - tk
        c = ((dl >= 0) & (dl <= 128)).astype(np.float32)
        c += ((dl >= 0) & (dl <= 512) & (dl % 4 == 0)).astype(np.float32)
        c += ((dl >= 0) & (dl <= 2048) & (dl % 16 == 0)).astype(np.float32)
        mall[:, k + 3, :] = c
    msb = np.zeros((128, 4, 4, 128), np.float32)
    for mrel in range(4):
        for qb in range(4):
            msb[:, mrel, qb, :] = ((128 * mrel + tk) < (128 * qb + tq)).astype(np.float32)
    return ident, tri, slow, mall.reshape(128, -1), msb.reshape(128, -1)


def _host_inputs(inputs, NSLOT, cores):
    x = np.asarray(inputs["x"], np.float32)
    p = np.asarray(inputs["p"], np.float32)[0]
    WSP = 4 * NSLOT
    SW = 512 * WSP
    ident, tri, slow, mall, msb = _consts()
    shared = {
        "g_mix": np.asarray(inputs["g_mix"], np.float32)[0][None, :],
        "g_mlp": np.asarray(inputs["g_mlp"], np.float32)[0][None, :],
        "g_ple": np.asarray(inputs["g_ple"], np.float32)[0][None, :],
        "qn_gain": np.asarray(inputs["qn_gain"], np.float32)[0][None, :],
        "kn_gain": np.asarray(inputs["kn_gain"], np.float32)[0][None, :],
        "w_in": np.asarray(inputs["w_in"], np.float32)[0],
        "w_branch_a": np.asarray(inputs["w_branch_a"], np.float32)[0],
        "w_branch_b": np.asarray(inputs["w_branch_b"], np.float32)[0],
        "w_out": np.asarray(inputs["w_out"], np.float32)[0],
        "w_up": np.asarray(inputs["w_up"], np.float32)[0],
        "w_down": np.asarray(inputs["w_down"], np.float32)[0],
        "w_ple_gate": np.asarray(inputs["w_ple_gate"], np.float32)[0],
        "w_ple_proj": np.asarray(inputs["w_ple_proj"], np.float32)[0],
        "c_ident": ident, "c_tri": tri, "c_slow": slow, "c_mall": mall, "c_msb": msb,
    }
    inv = (np.float32(500000.0) ** (-np.arange(0, 32, 2, dtype=np.float32) / np.float32(32))).astype(np.float32)
    maps = []
    for (b, j) in cores:
        pad = (3 - j) * 512
        nreal = SW - pad
        xwin = np.zeros((SW, D), np.float32)
        xwin[pad:] = x[b, :nreal]
        pos = (np.arange(SW) - pad).astype(np.float32)
        ang = pos[:, None] * inv[None, :]
        cs = np.concatenate([np.cos(ang), np.sin(ang)], axis=1).astype(np.float32)
        own_rows = np.concatenate([np.arange((4 * i + j) * 512, (4 * i + j + 1) * 512) for i in range(NSLOT)])
        valid = np.zeros((128, 12, 128), np.float32)
        for kb in range(12):
            if kb * 128 >= pad:
                valid[:, kb, :] = 1.0
        m = dict(shared)
        m["xw"] = xwin
        m["p_own"] = np.ascontiguousarray(p[b, own_rows])
        m["cs_tab"] = cs
        m["c_valid"] = valid.reshape(128, -1)
        maps.append(m)
    return maps


_CACHE = {}


def kernel(**inputs):
    NSLOT = 4
    if "nc" not in _CACHE:
        _CACHE["nc"] = build(NSLOT)[0]
    nc = _CACHE["nc"]
    cores = [(c // 4, c % 4) for c in range(8)]
    maps = _host_inputs(inputs, NSLOT, cores)
    res = run_bass_kernel_spmd(nc, maps, core_ids=list(range(8)))
    x = np.asarray(inputs["x"])
    out = np.empty(x.shape, np.float32)
    for c, (b, j) in enumerate(cores):
        o = res.results[c]["out_own"]
        for i in range(NSLOT):
            s = 4 * i + j
            out[b, s * 512:(s + 1) * 512] = o[i * 512:(i + 1) * 512]
    return out
```
